# Optimizing a Trainium2 kernel written in Bass

```python
import jax, jax.numpy as jnp
from jax import lax
import numpy as np

D_MODEL = 1024
BATCH = 2
SEQ = 8192
DEPTH = 4
DEC_BATCH = 2
DEC_SEQ = 16384
PAST_LEN = 128

HEAD_DIM = 64
NA_HEADS = 8
SWA_HEADS = 8
SWA_KV_HEADS = 2
SWA_GROUP = SWA_HEADS // SWA_KV_HEADS
NA_WIDTH = NA_HEADS * HEAD_DIM
SWA_WIDTH = SWA_HEADS * HEAD_DIM
SWA_KV_WIDTH = SWA_KV_HEADS * HEAD_DIM
MIX_WIDTH = NA_WIDTH + SWA_WIDTH
IN_WIDTH = 3 * NA_WIDTH + SWA_WIDTH + 2 * SWA_KV_WIDTH
D_FF = 4 * D_MODEL
GRID_W = 64
NA_ROWS_MAX = 8
NA_COLS = 16
WINDOW = 128
BLOCK = 128
ROPE_THETA = 10000.0
EPS = 1e-5
NEG = -1e30

kernel_name = "hybrid_na_swa_sink_encoder"


def rms_norm(x, g):
    xf = x.astype(jnp.float32)
    y = xf * lax.rsqrt(jnp.mean(xf * xf, axis=-1, keepdims=True) + EPS)
    return (y * g.astype(jnp.float32)).astype(x.dtype)


def rope(x):
    s = x.shape[1]
    half = HEAD_DIM // 2
    inv = ROPE_THETA ** (-jnp.arange(half, dtype=jnp.float32) / half)
    ang = jnp.arange(s, dtype=jnp.float32)[:, None] * inv[None, :]
    cos = jnp.cos(ang)[None, :, None, :]
    sin = jnp.sin(ang)[None, :, None, :]
    xf = x.astype(jnp.float32)
    x1, x2 = xf[..., :half], xf[..., half:]
    return jnp.concatenate([x1 * cos - x2 * sin, x2 * cos + x1 * sin], axis=-1).astype(x.dtype)


def neighbourhood_attention(q, k, v, rpb):
    b, s, h, dh = q.shape
    rows = s // GRID_W
    wr = min(NA_ROWS_MAX, rows)
    wc = NA_COLS
    qg = q.reshape(b, rows, GRID_W, h, dh)
    kg = k.reshape(b, rows, GRID_W, h, dh)
    vg = v.reshape(b, rows, GRID_W, h, dh)
    cols = np.arange(GRID_W)
    col_start = np.clip(cols - wc // 2, 0, GRID_W - wc)
    col_idx = col_start[:, None] + np.arange(wc)[None, :]
    dc_idx = col_idx - cols[:, None] + (NA_COLS - 1)
    scale = dh ** -0.5
    rpb_f = rpb.astype(jnp.float32)

    def one_row(r):
        rs = jnp.clip(r - wr // 2, 0, rows - wr)
        q_r = lax.dynamic_index_in_dim(qg, r, axis=1, keepdims=False)
        k_rows = lax.dynamic_slice_in_dim(kg, rs, wr, axis=1)
        v_rows = lax.dynamic_slice_in_dim(vg, rs, wr, axis=1)
        k_win = k_rows[:, :, col_idx]
        v_win = v_rows[:, :, col_idx]
        dr_idx = rs + jnp.arange(wr) - r + (NA_ROWS_MAX - 1)
        bias = rpb_f[:, dr_idx][:, :, dc_idx]
        sc = jnp.einsum('bchd,brcjhd->bhcrj', q_r, k_win,
                        preferred_element_type=jnp.float32) * scale
        sc = sc + jnp.transpose(bias, (0, 2, 1, 3))[None]
        p = jax.nn.softmax(sc.reshape(b, h, GRID_W, wr * wc), axis=-1)
        p = p.reshape(b, h, GRID_W, wr, wc).astype(v.dtype)
        return jnp.einsum('bhcrj,brcjhd->bchd', p, v_win)

    out = lax.map(one_row, jnp.arange(rows))
    return jnp.transpose(out, (1, 0, 2, 3, 4)).reshape(b, s, h, dh)


def sliding_window_attention(q, k, v, sink):
    b, s, hq, dh = q.shape
    hkv = k.shape[2]
    g = hq // hkv
    nb = s // BLOCK
    scale = dh ** -0.5
    qb = q.reshape(b, nb, BLOCK, hkv, g, dh)

    def band(t):
        pad = jnp.zeros((b, BLOCK, hkv, dh), t.dtype)
        tp = jnp.concatenate([pad, t, pad], axis=1).reshape(b, nb + 2, BLOCK, hkv, dh)
        return jnp.concatenate([tp[:, :-2], tp[:, 1:-1], tp[:, 2:]], axis=2)

    kb, vb = band(k), band(v)
    sc = jnp.einsum('bnqkgd,bnjkd->bnkgqj', qb, kb,
                    preferred_element_type=jnp.float32) * scale
    blk = jnp.arange(nb)
    qpos = blk[:, None] * BLOCK + jnp.arange(BLOCK)[None, :]
    kpos = (blk[:, None] - 1) * BLOCK + jnp.arange(3 * BLOCK)[None, :]
    diff = qpos[:, :, None] - kpos[:, None, :]
    valid = (jnp.abs(diff) <= WINDOW) & (kpos[:, None, :] >= 0) & (kpos[:, None, :] < s)
    sc = jnp.where(valid[None, :, None, None], sc, NEG)
    sink_l = sink.astype(jnp.float32).reshape(hkv, g)[None, None, :, :, None, None]
    m = jnp.maximum(jnp.max(sc, axis=-1, keepdims=True), sink_l)
    p = jnp.exp(sc - m)
    denom = jnp.sum(p, axis=-1, keepdims=True) + jnp.exp(sink_l - m)
    out = jnp.einsum('bnkgqj,bnjkd->bnqkgd', (p / denom).astype(v.dtype), vb)
    return out.reshape(b, s, hq, dh)


def mixer(h, w_in, rpb, sink, w_out):
    b, s, _ = h.shape
    proj = h @ w_in
    o1 = NA_WIDTH
    o2 = 2 * NA_WIDTH
    o3 = 3 * NA_WIDTH
    o4 = o3 + SWA_WIDTH
    o5 = o4 + SWA_KV_WIDTH
    qa, ka, va, qs, ks, vs = jnp.split(proj, [o1, o2, o3, o4, o5], axis=-1)
    qa = qa.reshape(b, s, NA_HEADS, HEAD_DIM)
    ka = ka.reshape(b, s, NA_HEADS, HEAD_DIM)
    va = va.reshape(b, s, NA_HEADS, HEAD_DIM)
    qs = rope(qs.reshape(b, s, SWA_HEADS, HEAD_DIM))
    ks = rope(ks.reshape(b, s, SWA_KV_HEADS, HEAD_DIM))
    vs = vs.reshape(b, s, SWA_KV_HEADS, HEAD_DIM)
    oa = neighbourhood_attention(qa, ka, va, rpb).reshape(b, s, NA_WIDTH)
    ob = sliding_window_attention(qs, ks, vs, sink).reshape(b, s, SWA_WIDTH)
    return jnp.concatenate([oa, ob], axis=-1) @ w_out


def trunk(x, norm_mix, w_in, rpb, sink, w_out, norm_mlp, w_up, w_down, norm_final):
    for l in range(DEPTH):
        x = x + mixer(rms_norm(x, norm_mix[l]), w_in[l], rpb[l], sink[l], w_out[l])
        hdn = rms_norm(x, norm_mlp[l]) @ w_up[l]
        x = x + jnp.square(jax.nn.relu(hdn)) @ w_down[l]
    return rms_norm(x, norm_final)


def setup_inputs(seed: int = 0) -> dict:
    key = jax.random.key(seed)
    ks = jax.random.split(key, 12)
    f32 = jnp.float32
    x_prompt = jax.random.normal(ks[0], (BATCH, SEQ, D_MODEL), f32)
    x_sample = jax.random.normal(ks[1], (DEC_BATCH, DEC_SEQ, D_MODEL), f32)
    norm_mix = 1.0 + 0.05 * jax.random.normal(ks[2], (DEPTH, D_MODEL), f32)
    w_in = jax.random.normal(ks[3], (DEPTH, D_MODEL, IN_WIDTH), f32) * D_MODEL ** -0.5
    rpb = 0.1 * jax.random.normal(ks[4], (DEPTH, NA_HEADS, 2 * NA_ROWS_MAX - 1, 2 * NA_COLS - 1), f32)
    sink = 0.5 * jax.random.normal(ks[5], (DEPTH, SWA_HEADS), f32)
    w_out = jax.random.normal(ks[6], (DEPTH, MIX_WIDTH, D_MODEL), f32) * MIX_WIDTH ** -0.5
    norm_mlp = 1.0 + 0.05 * jax.random.normal(ks[7], (DEPTH, D_MODEL), f32)
    w_up = jax.random.normal(ks[8], (DEPTH, D_MODEL, D_FF), f32) * D_MODEL ** -0.5
    w_down = jax.random.normal(ks[9], (DEPTH, D_FF, D_MODEL), f32) * D_FF ** -0.5
    norm_final = 1.0 + 0.05 * jax.random.normal(ks[10], (D_MODEL,), f32)
    return {"x_prompt": x_prompt, "x_sample": x_sample, "norm_mix": norm_mix, "w_in": w_in,
            "rpb": rpb, "sink": sink, "w_out": w_out, "norm_mlp": norm_mlp, "w_up": w_up,
            "w_down": w_down, "norm_final": norm_final}


def reference(x_prompt, x_sample, norm_mix, w_in, rpb, sink, w_out, norm_mlp, w_up, w_down, norm_final):
    y_prompt = trunk(x_prompt, norm_mix, w_in, rpb, sink, w_out, norm_mlp, w_up, w_down, norm_final)
    y_sample = trunk(x_sample, norm_mix, w_in, rpb, sink, w_out, norm_mlp, w_up, w_down, norm_final)
    return (y_prompt, y_sample)
```

```python
import os
import numpy as np
from contextlib import ExitStack
import concourse.bass as bass
import concourse.mybir as mybir
from concourse.bass_utils import run_bass_kernel_spmd

F32 = mybir.dt.float32
BF16 = mybir.dt.bfloat16
AF = mybir.ActivationFunctionType
ALU = mybir.AluOpType

D = 1024
NB = 64
HALO = 8
CORE_NB = 48
NEGV = -30000.0
BOUNDS = (8, 24, 40, 56)
SEQ_STARTS = (0, 8192, 16384, 32768)
T_ALL = 49152
EPS = 1e-5
NLAYERS = 4

QUEUES = ("pe", "act", "dve", "pool", "sp")
SAME_ENGINE_SYNC = {"act", "dve", "pool"}
EPOCH = 20000


class Prog:
    def __init__(self):
        self.ops = {q: [] for q in QUEUES}
        self.state = {}
        self.waited = {q: {} for q in QUEUES}
        self.dma_count = {}
        self.psum = set()

    def _deps(self, reads, writes):
        deps = {}
        for k in reads:
            st = self.state.get(k)
            if st:
                for s, i in st["w"].items():
                    if deps.get(s, -1) < i:
                        deps[s] = i
        for k in writes:
            st = self.state.get(k)
            if st:
                for s, i in st["w"].items():
                    if deps.get(s, -1) < i:
                        deps[s] = i
                for s, i in st["r"].items():
                    if deps.get(s, -1) < i:
                        deps[s] = i
        return deps

    def op(self, q, fn, reads=(), writes=(), dma_slot=None):
        pk = self.psum
        if q != "pe" and any(k in pk for k in reads):
            writes = list(writes) + [k for k in reads if k in pk]
            reads = [k for k in reads if k not in pk]
        deps = self._deps(reads, writes)
        waits = []
        wd = self.waited[q]
        for s, i in deps.items():
            if s == q and q not in SAME_ENGINE_SYNC:
                continue
            if wd.get(s, -1) >= i:
                continue
            wd[s] = i
            waits.append((s, i))
        idx = len(self.ops[q])
        rec = {"fn": fn, "waits": waits, "inc": False, "dma": dma_slot}
        if dma_slot is not None:
            n = self.dma_count.get(dma_slot, 0)
            self.dma_count[dma_slot] = n + 1
            tok = (("dma", dma_slot), n)
        else:
            tok = (q, idx)
        self.ops[q].append(rec)
        s, i = tok
        for k in reads:
            st = self.state.setdefault(k, {"w": {}, "r": {}})
            if st["r"].get(s, -1) < i:
                st["r"][s] = i
        for k in writes:
            self.state[k] = {"w": {s: i}, "r": {}}
        return tok

    def barrier(self, final=False):
        toks = []
        for q in QUEUES:
            for i in range(len(self.ops[q]) - 1, -1, -1):
                r = self.ops[q][i]
                if r["fn"] is not None and r["dma"] is None:
                    toks.append((q, i))
                    break
        for k, n in self.dma_count.items():
            toks.append((("dma", k), n - 1))
        for q in (("sp",) if final else QUEUES):
            waits = []
            wd = self.waited[q]
            for s, i in toks:
                if s == q:
                    continue
                if wd.get(s, -1) >= i:
                    continue
                wd[s] = i
                waits.append((s, i))
            self.ops[q].append({"fn": None, "waits": waits, "inc": False, "dma": None})
        self.state = {}

    def finalize(self):
        for q in QUEUES:
            for rec in self.ops[q]:
                for s, i in rec["waits"]:
                    if not isinstance(s, tuple):
                        self.ops[s][i]["inc"] = True
        self.val = {}
        for q in QUEUES:
            c = 0
            for i, rec in enumerate(self.ops[q]):
                if rec["inc"] and rec["dma"] is None:
                    c += 1
                    self.val[(q, i)] = c
        names = []
        for q in QUEUES:
            c = sum(1 for r in self.ops[q] if r["inc"] and r["dma"] is None)
            for e in range(c // EPOCH + 1):
                names.append((q, e))
        for k in self.dma_count:
            names.append(("dma", k))
        return names

    def run(self, q, eng, sems):
        def sv(tok):
            s, i = tok
            if isinstance(s, tuple):
                return sems[("dma", s[1])], 16 * (i + 1)
            v = self.val[(s, i)]
            e = (v - 1) // EPOCH
            return sems[(s, e)], v - e * EPOCH
        for i, rec in enumerate(self.ops[q]):
            for tok in rec["waits"]:
                sem, v = sv(tok)
                eng.wait_ge(sem, v)
            if rec["fn"] is None:
                continue
            ins = rec["fn"](eng)
            if rec["dma"] is not None:
                ins.then_inc(sems[("dma", rec["dma"])], 16)
            elif rec["inc"]:
                sem, v = sv((q, i))
                ins.then_inc(sem, 1)


NEGT = "neg"
NORMAL_WIN = [(-2, 7, 7), (-1, 2, 2), (0, 3, 3), (1, 4, 4), (2, 8, 8)]


def na_window(b):
    for p, B in enumerate(BOUNDS):
        if b == B:
            return [(b - 2, 7, NEGT), (b - 1, 2, NEGT), (b, 3, 3), (b + 1, 4, 4), (b + 2, 8, 5), (b + 3, NEGT, 6)], p
        if b == B + 1:
            return [(b - 2, 7, NEGT), (b - 1, 2, 2), (b, 3, 3), (b + 1, 4, 4), (b + 2, 8, 5)], p
        if b == B - 2:
            return [(b - 2, 7, 1), (b - 1, 2, 2), (b, 3, 3), (b + 1, 4, 4), (b + 2, 8, NEGT)], p
        if b == B - 1:
            return [(b - 3, NEGT, 0), (b - 2, 7, 1), (b - 1, 2, 2), (b, 3, 3), (b + 1, 4, NEGT), (b + 2, 8, NEGT)], p
    return [(b + o, tn, tc) for o, tn, tc in NORMAL_WIN], None


def build_program(nlayers=NLAYERS, debug_out=None):
    nc = bass.Bass("TRN2", target_bir_lowering=False)

    def din(name, shape):
        return nc.dram_tensor(name, shape, F32, kind="ExternalInput").ap()

    x_ext = din("x_ext", [NB * 128, D])
    w_in = din("w_in", [NLAYERS, D, 2304])
    w_out = din("w_out", [NLAYERS, D, D])
    w_up = din("w_up", [NLAYERS, D, 4096])
    w_down = din("w_down", [NLAYERS, 4096, D])
    gmix_d = din("gmix", [NLAYERS, D])
    gmlp_d = din("gmlp", [NLAYERS, D])
    gfin_d = din("gfin", [D])
    ubias_d = din("ubias", [NLAYERS, 128, 9, 8 * 128])
    sink_d = din("sink16", [NLAYERS * 16])
    rope_d = din("rope", [NB, 128, 256])
    cmask_d = din("cmask", [128, 3 * 512])
    seli_d = din("seli", [128, 8 * 128])
    ident_d = din("ident", [128, 128])
    y = nc.dram_tensor("y", [CORE_NB * 128, D], F32, kind="ExternalOutput").ap()
    xs = nc.dram_tensor("xs", [NB * 128, D], F32).ap()

    P = Prog()
    es = ExitStack()
    with es:
        def sb(name, shape, dt):
            return es.enter_context(nc.sbuf_tensor(name, shape, dt))

        def ps(name, shape, dt):
            return es.enter_context(nc.psum_tensor(name, shape, dt))

        ABF = 79872
        AF32 = 9344
        abf = sb("abf_sb", [128, ABF], BF16)
        af32 = sb("af32_sb", [128, AF32], F32)
        ident = sb("ident_sb", [128, 128], BF16)
        seli = sb("seli_sb", [128, 8, 128], BF16)
        cmask = sb("cmask_sb", [128, 3, 512], BF16)
        esink = sb("esink_sb", [128, NLAYERS * 16], F32)
        stat = sb("stat_sb", [128, 8], F32)
        den = sb("den_sb", [128, 16], F32)

        psT = ps("psT", [128, 8, 128], BF16)
        pb = [ps(f"pb{i}", [128, 512], F32) for i in range(7)]
        P.psum = {"psT"} | {f"pb{i}" for i in range(7)}

        def bfv(off, n):
            return abf[:, off:off + n]

        def f32v(off, n):
            return af32[:, off:off + n]

        o = 0
        WIN = bfv(o, 8 * 2304).rearrange("p (k n) -> p k n", n=2304); o += 8 * 2304
        WOUT = bfv(o, 8 * 1024).rearrange("p (k n) -> p k n", n=1024); o += 8 * 1024
        UT = bfv(o, 9 * 8 * 128).rearrange("p (t h q) -> p t h q", h=8, q=128); o += 9 * 8 * 128
        KR = []
        for s_ in range(8):
            KR.append(bfv(o, 5 * 128).rearrange("p (c t) -> p c t", t=128)); o += 5 * 128
        QR_OFF = o
        NQR = 6
        QBD, QSA, QSB = [], [], []
        for s_ in range(NQR):
            QBD.append(bfv(o, 1024).rearrange("p (i t) -> p i t", t=256)); o += 1024
            QSA.append(bfv(o, 512).rearrange("p (a t) -> p a t", t=128)); o += 512
            QSB.append(bfv(o, 512).rearrange("p (a t) -> p a t", t=128)); o += 512
        QR_ALL = bfv(QR_OFF, NQR * 2048)
        VR = []
        for s_ in range(8):
            VR.append(bfv(o, 10 * 65).rearrange("p (h d) -> p h d", d=65)); o += 10 * 65
        HB = [bfv(o + i * 1024, 1024) for i in range(2)]; o += 2048
        HT = [bfv(o + i * 1024, 1024).rearrange("p (k t) -> p k t", t=128) for i in range(2)]; o += 2048
        QKTOK = [bfv(o + i * 1664, 1664) for i in range(2)]; o += 2 * 1664
        PT = bfv(o, 12 * 512).rearrange("p (c t) -> p c t", t=512); o += 12 * 512
        PTS = bfv(o, 6 * 512).rearrange("p (c t) -> p c t", t=512); o += 6 * 512
        ATT = [bfv(o + i * 1024, 1024) for i in range(2)]; o += 2048
        ATT_T = [bfv(o + i * 1024, 1024).rearrange("p (k t) -> p k t", t=128) for i in range(2)]; o += 2048
        assert o <= ABF, o
        o = 0
        WUP = bfv(o, 8 * 4096).rearrange("p (k n) -> p k n", n=4096); o += 8 * 4096
        WDN = bfv(o, 32 * 1024).rearrange("p (k n) -> p k n", n=1024); o += 32 * 1024
        HID = bfv(o, 32 * 256).rearrange("p (k t) -> p k t", t=256); o += 32 * 256
        H2 = [bfv(o + i * 1024, 1024) for i in range(2)]; o += 2048
        H2T = [bfv(o + i * 2048, 2048).rearrange("p (k t) -> p k t", t=256) for i in range(2)]; o += 4096
        assert o <= ABF, o
        o = 0
        STG = [f32v(o + i * 1152, 1152) for i in range(3)]; o += 3 * 1152
        o1 = o
        XP = [f32v(o + i * 1024, 1024) for i in range(2)]; o += 2048
        RP = [f32v(o + i * 256, 256).rearrange("p (a d) -> p a d", d=64) for i in range(3)]; o += 768
        RTA = f32v(o, 512).rearrange("p (h d) -> p h d", d=64); o += 512
        RTB = f32v(o, 512).rearrange("p (h d) -> p h d", d=64); o += 512
        XO = [f32v(o + i * 1024, 1024) for i in range(2)]; o += 2048
        assert o <= AF32, o
        o = o1
        XM = [f32v(o + i * 2048, 2048).rearrange("p (k n) -> p k n", n=1024) for i in range(2)]; o += 4096
        RTMP = [f32v(o + i * 256, 256) for i in range(2)]; o += 512
        GFIN = f32v(o, 1024); o += 1024
        assert o <= AF32, o

        GB = f32v(0, 1024)
        stg_ctr = [0]

        def wload(dmas, keys, slot="wl"):
            tok = None
            for out_ap, in_ap in dmas:
                tok = P.op("pool", lambda e, out_ap=out_ap, in_ap=in_ap: e.dma_start(out=out_ap, in_=in_ap), dma_slot=slot)
            for k in keys:
                P.state[k] = {"w": {tok[0]: tok[1]}, "r": {}}

        def load_cast(src_ap, n, dst_ap, dst_key, scale_ap=None, scale_key=None):
            i = stg_ctr[0]
            stg_ctr[0] += 1
            slot = i % 3
            st = STG[slot][:, :n]
            P.op("sp", lambda e: e.dma_start(out=st, in_=src_ap), writes=[f"stg{slot}"], dma_slot=f"stg{slot}")
            eng = ("pool", "act", "dve")[i % 3]
            rd = [f"stg{slot}"] + ([scale_key] if scale_key else [])
            if eng == "act":
                if scale_ap is None:
                    P.op("act", lambda e: e.copy(out=dst_ap, in_=st), reads=rd, writes=[dst_key])
                else:
                    P.op("act", lambda e: e.activation(out=dst_ap, in_=st, func=AF.Copy, scale=scale_ap), reads=rd, writes=[dst_key])
            else:
                if scale_ap is None:
                    P.op(eng, lambda e: e.tensor_copy(out=dst_ap, in_=st), reads=rd, writes=[dst_key])
                else:
                    P.op(eng, lambda e: e.tensor_scalar(out=dst_ap, in0=st, scalar1=scale_ap, scalar2=None, op0=ALU.mult), reads=rd, writes=[dst_key])

        load_cast(ident_d, 128, ident[:], "ident")
        load_cast(seli_d, 1024, seli[:].rearrange("p a t -> p (a t)"), "seli")
        for j in range(3):
            load_cast(cmask_d[:, j * 512:(j + 1) * 512], 512, cmask[:, j, :], f"cmask{j}")
        P.op("sp", lambda e: e.dma_start(out=esink[:], in_=sink_d.partition_broadcast(128)), writes=["esink"], dma_slot="c2")
        P.op("act", lambda e: e.activation(out=esink[:], in_=esink[:], func=AF.Exp), reads=["esink"], writes=["esink"])
        NEG512 = cmask[:, 2, :]
        NEG128 = cmask[:, 2, 0:128]

        def srckey(l, b):
            return f"xin{b}" if l == 0 else f"xs{b}"

        shift = NLAYERS - nlayers
        pre_win = [False]
        pre_wup = [False]

        def phase1(l):
            lo, hi = 2 + 2 * (l + shift), 62 - 2 * (l + shift)
            klo, khi = lo - 2, hi + 2
            src = x_ext if l == 0 else xs
            dm = []
            for k2 in range(4):
                dm.append((WIN[:, 2 * k2:2 * k2 + 2, :], w_in[l, k2 * 256:(k2 + 1) * 256, :].rearrange("(k p) n -> p k n", p=128)))
            for k4 in range(2):
                dm.append((WOUT[:, 4 * k4:4 * k4 + 4, :], w_out[l, k4 * 512:(k4 + 1) * 512, :].rearrange("(k p) n -> p k n", p=128)))
            if pre_win[0]:
                for kc in range(8):
                    P.state[f"win{kc}"] = {"w": {("dma", "wl0"): P.dma_count["wl0"] - 1}, "r": {}}
                pre_win[0] = False
            else:
                wload(dm[:4], [f"win{kc}" for kc in range(8)], "wl0")
            wload(dm[4:] + [(UT[:].rearrange("p t h q -> p (t h q)"), ubias_d[l].rearrange("p t n -> p (t n)"))],
                  [f"wout{kc}" for kc in range(8)] + [f"ut{t}" for t in range(9)], "wl1")
            P.op("sp", lambda e: e.dma_start(out=GB, in_=gmix_d[l].partition_broadcast(128)), writes=["gb", "stg0"], dma_slot="c0")
            P.op("pool", lambda e: e.memset(QR_ALL, 0.0), writes=[f"qr{i}{x}" for i in range(NQR) for x in "abcd"])
            for s in range(8):
                P.op("pool", lambda e, s=s: e.memset(VR[s][:, :, 64:65], 1.0), writes=[f"vr1_{s}"])

            def proj_a(p):
                xi, hi_, ri = p % 2, p % 2, p % 3
                P.op("sp", lambda e: e.dma_start(out=XP[xi], in_=src[p * 128:(p + 1) * 128, :]),
                     reads=[srckey(l, p)], writes=[f"xp{xi}"], dma_slot=f"xp{xi}")
                P.op("sp", lambda e: e.dma_start(out=RP[ri].rearrange("p a d -> p (a d)"), in_=rope_d[p]),
                     writes=[f"rp{ri}"], dma_slot=f"rp{ri}")
                sc = stat[:, xi:xi + 1]
                rc = stat[:, 2 + xi:3 + xi]
                P.op("act", lambda e: e.activation(out=HB[hi_], in_=XP[xi], func=AF.Square, accum_out=sc),
                     reads=[f"xp{xi}"], writes=[f"hb{hi_}", f"ss{xi}"])
                P.op("act", lambda e: e.activation(out=rc, in_=sc, func=AF.Ln, scale=1.0 / D, bias=EPS),
                     reads=[f"ss{xi}"], writes=[f"rs{xi}"])
                P.op("act", lambda e: e.activation(out=rc, in_=rc, func=AF.Exp, scale=-0.5),
                     reads=[f"rs{xi}"], writes=[f"rs{xi}"])
                P.op("dve", lambda e: e.scalar_tensor_tensor(out=HB[hi_], in0=XP[xi], scalar=rc, in1=GB, op0=ALU.mult, op1=ALU.mult),
                     reads=[f"xp{xi}", f"rs{xi}", "gb"], writes=[f"hb{hi_}"])

            SCR = [(pb[2][:], pb[2][:].bitcast(BF16).rearrange("p (c t) -> p c t", t=128), "pb2"),
                   (pb[3][:], pb[3][:].bitcast(BF16).rearrange("p (c t) -> p c t", t=128), "pb3"),
                   (psT[:].rearrange("p c t -> p (c t)").bitcast(F32), psT[:], "psT")]

            def scr():
                i = sbank[0] % 2
                sbank[0] += 1
                return SCR[i]

            def tscr():
                return SCR[2]

            def proj_b(p):
                hi_ = p % 2
                _, tv, tk = tscr()
                for kc in range(8):
                    P.op("pe", lambda e, kc=kc: e.transpose(out=tv[:, kc, :], in_=HB[hi_][:, kc * 128:(kc + 1) * 128], identity=ident[:]),
                         reads=[f"hb{hi_}", "ident"], writes=[tk])
                P.op("dve", lambda e: e.tensor_copy(out=HT[hi_], in_=tv), reads=[tk], writes=[f"ht{hi_}"])

            def proj_c(p, early_d1=False, only=None):
                hi_, ri, slot, qi = p % 2, p % 3, p % 8, p % 2
                qk = QKTOK[qi]
                groups = [(0, 512), (512, 512), (1024, 512), (1536, 512), (2048, 256)]
                for gi, (c0, n) in enumerate(groups):
                    if only is not None and gi != only:
                        continue
                    bank = gi % 2
                    pp = pb[bank]
                    for kc in range(8):
                        P.op("pe", lambda e, kc=kc, pp=pp, c0=c0, n=n: e.matmul(pp[:, :n], lhsT=HT[hi_][:, kc, :], rhs=WIN[:, kc, c0:c0 + n],
                                                                              start=(kc == 0), stop=(kc == 7)),
                             reads=[f"ht{hi_}", f"win{kc}"], writes=[f"pb{bank}"])
                    if gi == 0:
                        P.op("dve", lambda e, pp=pp: e.tensor_scalar(out=qk[:, 0:512], in0=pp[:], scalar1=0.125, scalar2=None, op0=ALU.mult),
                             reads=[f"pb{bank}"], writes=[f"qk{qi}a"])
                    elif gi == 1:
                        P.op("dve", lambda e, pp=pp: e.tensor_copy(out=qk[:, 512:1024], in_=pp[:]),
                             reads=[f"pb{bank}"], writes=[f"qk{qi}b"])
                    elif gi == 2:
                        P.op("act", lambda e, pp=pp: e.copy(out=VR[slot][:, 0:8, 0:64], in_=pp[:].rearrange("p (h d) -> p h d", d=64)),
                             reads=[f"pb{bank}"], writes=[f"vra{slot}"])
                        if early_d1:
                            proj_d1(p)
                    elif gi == 3:
                        xv = pp[:].rearrange("p (h d) -> p h d", d=64)
                        cc = RP[ri][:, 0:1, :].to_broadcast([128, 8, 64])
                        ss1 = RP[ri][:, 1:2, 0:32].to_broadcast([128, 8, 32])
                        ss2 = RP[ri][:, 1:2, 32:64].to_broadcast([128, 8, 32])
                        P.op("dve", lambda e, xv=xv, cc=cc: e.tensor_tensor(out=RTA[:], in0=xv, in1=cc, op=ALU.mult),
                             reads=[f"pb{bank}", f"rp{ri}"], writes=["rta"])
                        P.op("dve", lambda e, xv=xv, ss1=ss1: e.tensor_tensor(out=RTB[:, :, 0:32], in0=xv[:, :, 32:64], in1=ss1, op=ALU.mult),
                             reads=[f"pb{bank}", f"rp{ri}"], writes=["rtb1"])
                        P.op("dve", lambda e, xv=xv, ss2=ss2: e.tensor_tensor(out=RTB[:, :, 32:64], in0=xv[:, :, 0:32], in1=ss2, op=ALU.mult),
                             reads=[f"pb{bank}", f"rp{ri}"], writes=["rtb2"])
                        P.op("pool", lambda e: e.tensor_tensor(out=qk[:, 1024:1536].rearrange("p (h d) -> p h d", d=64), in0=RTA[:], in1=RTB[:], op=ALU.add),
                             reads=["rta", "rtb1", "rtb2"], writes=[f"qk{qi}c"])
                    else:
                        P.op("act", lambda e, pp=pp: e.copy(out=VR[slot][:, 8:10, 0:64], in_=pp[:, 128:256].rearrange("p (h d) -> p h d", d=64)),
                             reads=[f"pb{bank}"], writes=[f"vrs{slot}"])
                        xv = pp[:, 0:128].rearrange("p (h d) -> p h d", d=64)
                        cc = RP[ri][:, 2:3, :].to_broadcast([128, 2, 64])
                        ss1 = RP[ri][:, 3:4, 0:32].to_broadcast([128, 2, 32])
                        ss2 = RP[ri][:, 3:4, 32:64].to_broadcast([128, 2, 32])
                        P.op("dve", lambda e, xv=xv, cc=cc: e.tensor_tensor(out=RTA[:, 0:2, :], in0=xv, in1=cc, op=ALU.mult),
                             reads=[f"pb{bank}", f"rp{ri}"], writes=["rta"])
                        P.op("dve", lambda e, xv=xv, ss1=ss1: e.tensor_tensor(out=RTB[:, 0:2, 0:32], in0=xv[:, :, 32:64], in1=ss1, op=ALU.mult),
                             reads=[f"pb{bank}", f"rp{ri}"], writes=["rtb1"])
                        P.op("dve", lambda e, xv=xv, ss2=ss2: e.tensor_tensor(out=RTB[:, 0:2, 32:64], in0=xv[:, :, 0:32], in1=ss2, op=ALU.mult),
                             reads=[f"pb{bank}", f"rp{ri}"], writes=["rtb2"])
                        P.op("pool", lambda e: e.tensor_tensor(out=qk[:, 1536:1664].rearrange("p (h d) -> p h d", d=64), in0=RTA[:, 0:2, :], in1=RTB[:, 0:2, :], op=ALU.add),
                             reads=["rta", "rtb1", "rtb2"], writes=[f"qk{qi}d"])

            def proj_d1(p):
                slot, qi, qs = p % 8, p % 2, p % NQR
                qk = QKTOK[qi]
                _, tv, tk = tscr()
                for c in range(8):
                    P.op("pe", lambda e, c=c: e.transpose(out=tv[:, c, :], in_=qk[:, c * 128:(c + 1) * 128], identity=ident[:]),
                         reads=[f"qk{qi}a", f"qk{qi}b", "ident"], writes=[tk])
                P.op("dve", lambda e: e.tensor_copy(out=QBD[qs][0:64, :, 0:128], in_=tv[0:64, 0:4, :]), reads=[tk], writes=[f"qr{qs}a"])
                P.op("dve", lambda e: e.tensor_copy(out=QBD[qs][64:128, :, 128:256], in_=tv[64:128, 0:4, :]), reads=[tk], writes=[f"qr{qs}b"])
                P.op("dve", lambda e: e.tensor_copy(out=KR[slot][:, 0:4, :], in_=tv[:, 4:8, :]), reads=[tk], writes=[f"kr{slot}a"])

            def proj_d2(p):
                slot, qi, qs = p % 8, p % 2, p % NQR
                qk = QKTOK[qi]
                _, pT2, tk = scr()
                for a in range(4):
                    P.op("pe", lambda e, a=a: e.transpose(out=pT2[:, a, :], in_=qk[:, 1024 + a * 128:1024 + (a + 1) * 128], identity=ident[:]),
                         reads=[f"qk{qi}c", "ident"], writes=[tk])
                P.op("pe", lambda e: e.transpose(out=pT2[:, 4, :], in_=qk[:, 1536:1664], identity=ident[:]),
                     reads=[f"qk{qi}d", "ident"], writes=[tk])
                P.op("act", lambda e: e.copy(out=QSA[qs][0:64, :, :], in_=pT2[0:64, 0:4, :]), reads=[tk], writes=[f"qr{qs}c"])
                P.op("act", lambda e: e.copy(out=QSB[qs][64:128, :, :], in_=pT2[64:128, 0:4, :]), reads=[tk], writes=[f"qr{qs}d"])
                P.op("act", lambda e: e.copy(out=KR[slot][:, 4, :], in_=pT2[:, 4, :]), reads=[tk], writes=[f"kr{slot}b"])

            def proj_d(p):
                proj_d1(p)
                proj_d2(p)

            sbank = [0]

            ostart = [set()]

            def pv(oo_bank, oo, lhsT, rhs, rkeys):
                st = oo_bank not in ostart[0]
                ostart[0].add(oo_bank)
                P.op("pe", lambda e: e.matmul(oo, lhsT=lhsT, rhs=rhs, start=st, stop=True, skip_group_check=True),
                     reads=rkeys, writes=[f"pb{oo_bank}"])

            def attn_na(b, mid=None):
                ostart[0] = set()
                win, pidx = na_window(b)
                nch = len(win)
                qs = b % NQR
                groups = [(j, qd) for j in range(nch) for qd in range(2)]

                def s_group(gi):
                    j, qd = groups[gi]
                    kb, tn, tcl = win[j]
                    ks = kb % 8
                    S, _, bkey = scr()
                    for pi in range(2):
                        i = 2 * qd + pi
                        P.op("pe", lambda e, S=S, pi=pi, i=i, ks=ks: e.matmul(S[:, pi * 256:(pi + 1) * 256], lhsT=KR[ks][:, i, :], rhs=QBD[qs][:, i, :],
                                                                             start=(pi == 0), stop=False),
                             reads=[f"kr{ks}a", f"qr{qs}a", f"qr{qs}b"], writes=[bkey])

                    def tab(t):
                        return NEG512 if t == NEGT else UT[:, t, 4 * qd:4 * qd + 4, :].rearrange("p h q -> p (h q)")

                    def tkey(t):
                        return "cmask2" if t == NEGT else f"ut{t}"
                    if tn == tcl:
                        P.op("pe", lambda e, S=S, r=tab(tn): e.matmul(S[:], lhsT=ident[:], rhs=r, start=False, stop=True),
                             reads=["ident", tkey(tn)], writes=[bkey])
                    else:
                        P.op("pe", lambda e, S=S, r=tab(tn): e.matmul(S[:], lhsT=seli[:, 2 * pidx, :], rhs=r, start=False, stop=False),
                             reads=["seli", tkey(tn)], writes=[bkey])
                        P.op("pe", lambda e, S=S, r=tab(tcl): e.matmul(S[:], lhsT=seli[:, 2 * pidx + 1, :], rhs=r, start=False, stop=True),
                             reads=["seli", tkey(tcl)], writes=[bkey])
                    P.op("act", lambda e, S=S, gi=gi: e.activation(out=PT[:, gi, :], in_=S[:], func=AF.Exp),
                         reads=[bkey], writes=[f"pt{gi}"])

                def pv_group(gi):
                    j, qd = groups[gi]
                    ks = win[j][0] % 8
                    for hq in range(4):
                        h = 4 * qd + hq
                        ob = 4 + h // 7
                        oo = pb[ob][:, (h % 7) * 65:(h % 7) * 65 + 65]
                        pv(ob, oo, PT[:, gi, hq * 128:(hq + 1) * 128], VR[ks][:, h, :], [f"pt{gi}", f"vra{ks}", f"vr1_{ks}"])

                ng = len(groups)
                for gi in range(ng):
                    s_group(gi)
                    if gi >= 1:
                        pv_group(gi - 1)
                    if mid is not None and gi in mid:
                        mid[gi]()
                pv_group(ng - 1)

            def attn_swa(b, hook=None):
                qs = b % NQR
                p_first = [pi for pi, B in enumerate(BOUNDS) if b == B]
                p_last = [pi for pi, B in enumerate(BOUNDS) if b == B - 1]
                groups = [(g, j) for g in range(2) for j in range(3)]

                def s_group(gi):
                    g, j = groups[gi]
                    kb = b - 1 + j
                    ks = kb % 8
                    S, _, bkey = scr()
                    extra = []
                    if j == 0:
                        extra.append((ident[:], cmask[:, 0, :], ["ident", "cmask0"]))
                        if p_first:
                            extra.append((seli[:, 2 * p_first[0] + 1, :], NEG512, ["seli", "cmask2"]))
                    if j == 2:
                        extra.append((ident[:], cmask[:, 1, :], ["ident", "cmask1"]))
                        if p_last:
                            extra.append((seli[:, 2 * p_last[0] + 1, :], NEG512, ["seli", "cmask2"]))
                    qsrc = (QSA if g == 0 else QSB)[qs][:].rearrange("p a t -> p (a t)")
                    P.op("pe", lambda e, S=S, ks=ks, qsrc=qsrc, last=(len(extra) == 0): e.matmul(S[:], lhsT=KR[ks][:, 4, :], rhs=qsrc, start=True, stop=last),
                         reads=[f"kr{ks}b", f"qr{qs}c", f"qr{qs}d"], writes=[bkey])
                    for xi, (lt, rh, rk) in enumerate(extra):
                        P.op("pe", lambda e, S=S, lt=lt, rh=rh, last=(xi == len(extra) - 1): e.matmul(S[:], lhsT=lt, rhs=rh, start=False, stop=last),
                             reads=rk, writes=[bkey])
                    P.op("act", lambda e, S=S, gi=gi: e.activation(out=PTS[:, gi, :], in_=S[:], func=AF.Exp),
                         reads=[bkey], writes=[f"pts{gi}"])

                def pv_group(gi):
                    g, j = groups[gi]
                    ks = (b - 1 + j) % 8
                    for a in range(4):
                        hh = 8 + 4 * g + a
                        ob = 4 + hh // 7
                        oo = pb[ob][:, (hh % 7) * 65:(hh % 7) * 65 + 65]
                        pv(ob, oo, PTS[:, gi, a * 128:(a + 1) * 128], VR[ks][:, 8 + g, :], [f"pts{gi}", f"vrs{ks}", f"vr1_{ks}"])

                for gi in range(6):
                    s_group(gi)
                    if gi >= 1:
                        pv_group(gi - 1)
                    if hook is not None:
                        hook(gi)
                pv_group(5)

            def attn_norm(b):
                ai = b % 2
                at = ATT[ai].rearrange("p (h d) -> p h d", d=64)
                for ob, h0, nh in ((4, 0, 7), (5, 7, 7), (6, 14, 2)):
                    ov = pb[ob][:, 0:nh * 65].rearrange("p (h d) -> p h d", d=65)
                    dn = den[:, h0:h0 + nh]
                    P.op("dve", lambda e, ov=ov, dn=dn, h0=h0, nh=nh: e.tensor_tensor(out=dn, in0=ov[:, :, 64], in1=esink[:, l * 16 + h0:l * 16 + h0 + nh], op=ALU.add),
                         reads=[f"pb{ob}", "esink"], writes=[f"den{ob}"])
                    P.op("dve", lambda e, dn=dn: e.reciprocal(out=dn, in_=dn), reads=[f"den{ob}"], writes=[f"den{ob}"])
                    P.op("dve", lambda e, ov=ov, dn=dn, h0=h0, nh=nh: e.tensor_tensor(out=at[:, h0:h0 + nh, :], in0=ov[:, :, 0:64],
                                                                                     in1=dn.unsqueeze(2).to_broadcast([128, nh, 64]), op=ALU.mult),
                         reads=[f"pb{ob}", f"den{ob}"], writes=[f"att{ai}_{ob}"])

            def attn_xo_load(b):
                xi = b % 2
                P.op("sp", lambda e: e.dma_start(out=XO[xi], in_=src[b * 128:(b + 1) * 128, :]),
                     reads=[srckey(l, b)], writes=[f"xo{xi}"], dma_slot=f"xol{xi}")

            def attn_out_t(b):
                ai = b % 2
                _, tv, tk = tscr()
                for kc in range(8):
                    P.op("pe", lambda e, kc=kc: e.transpose(out=tv[:, kc, :], in_=ATT[ai][:, kc * 128:(kc + 1) * 128], identity=ident[:]),
                         reads=[f"att{ai}_4", f"att{ai}_5", f"att{ai}_6", "ident"], writes=[tk])
                P.op("dve", lambda e: e.tensor_copy(out=ATT_T[ai], in_=tv), reads=[tk], writes=[f"attT{ai}"])

            def attn_out_mm(b, hf):
                ai, xi = b % 2, b % 2
                pp = pb[hf]
                for kc in range(8):
                    P.op("pe", lambda e, kc=kc: e.matmul(pp[:], lhsT=ATT_T[ai][:, kc, :], rhs=WOUT[:, kc, hf * 512:(hf + 1) * 512],
                                                         start=(kc == 0), stop=(kc == 7)),
                         reads=[f"attT{ai}", f"wout{kc}"], writes=[f"pb{hf}"])
                P.op("dve", lambda e: e.tensor_tensor(out=XO[xi][:, hf * 512:(hf + 1) * 512], in0=pp[:], in1=XO[xi][:, hf * 512:(hf + 1) * 512], op=ALU.add),
                     reads=[f"pb{hf}", f"xo{xi}"], writes=[f"xo{xi}"])

            def attn_out_store(b):
                xi = b % 2
                P.op("pool", lambda e: e.dma_start(out=xs[b * 128:(b + 1) * 128, :], in_=XO[xi]),
                     reads=[f"xo{xi}"], writes=[f"xs{b}"], dma_slot=f"xos{xi}")

            def attn_out(b):
                attn_out_mm(b, 0)
                attn_out_mm(b, 1)
                attn_out_store(b)

            def proj(p):
                proj_a(p); proj_b(p); proj_c(p); proj_d(p)

            LA = 4
            pro = list(range(klo, min(lo + LA, khi)))
            seq = pro + ([lo + LA] if lo + LA < khi else [])
            proj_a(seq[0])
            proj_b(seq[0])
            for i, pp_ in enumerate(pro):
                nxt = seq[i + 1] if i + 1 < len(seq) else None
                if nxt is not None:
                    proj_a(nxt)
                proj_c(pp_, early_d1=True)
                if nxt is not None and nxt in pro:
                    proj_b(nxt)
                proj_d2(pp_)
            pend_d2 = [None]
            for b in range(lo, hi):
                p = b + LA
                dop = p < khi
                if dop:
                    proj_b(p)
                mid = {}
                if pend_d2[0] is not None:
                    mid[1] = (lambda q=pend_d2[0]: proj_d2(q))
                    pend_d2[0] = None
                if b > lo:
                    mid[3] = (lambda b=b: attn_out_t(b - 1))
                    mid[6] = (lambda b=b: attn_out_mm(b - 1, 0))
                    mid[8] = (lambda b=b: (attn_out_mm(b - 1, 1), attn_out_store(b - 1)))
                attn_na(b, mid=mid)
                attn_xo_load(b)
                if p + 1 < khi:
                    proj_a(p + 1)
                if dop:
                    proj_c(p, early_d1=True, only=0)
                    attn_swa(b, hook=(lambda gi, p=p: proj_c(p, early_d1=True, only=gi + 1) if gi < 4 else None))
                    pend_d2[0] = p
                    if p == khi - 1:
                        for kc in range(4):
                            P.op("pool", lambda e, kc=kc: e.dma_start(out=WUP[:, kc, :], in_=w_up[l, kc * 128:(kc + 1) * 128, :]),
                                 writes=[f"win{k}" for k in range(8)], dma_slot="wl0")
                        pre_wup[0] = True
                else:
                    attn_swa(b)
                attn_norm(b)
            if pend_d2[0] is not None:
                proj_d2(pend_d2[0])
            attn_out_t(hi - 1)
            attn_out(hi - 1)

        def phase2(l, last):
            lo, hi = 2 + 2 * (l + shift), 62 - 2 * (l + shift)
            dm = []
            for kc in range(8):
                dm.append((WUP[:, kc, :], w_up[l, kc * 128:(kc + 1) * 128, :]))
            for h4 in range(8):
                dm.append((WDN[:, 4 * h4:4 * h4 + 4, :], w_down[l, h4 * 512:(h4 + 1) * 512, :].rearrange("(k p) n -> p k n", p=128)))
            wload(dm[4:8] if pre_wup[0] else dm[:8], [f"wup{kc}" for kc in range(8)], "wl0")
            pre_wup[0] = False
            wload(dm[8:], [f"wdn{hc}" for hc in range(32)], "wl1")
            P.op("sp", lambda e: e.dma_start(out=GB, in_=gmlp_d[l].partition_broadcast(128)), writes=["gb", "stg0"], dma_slot="c1")
            if last:
                P.op("sp", lambda e: e.dma_start(out=GFIN, in_=gfin_d.partition_broadcast(128)), writes=["gfin"], dma_slot="c3")

            sbs = list(range(lo, hi, 2))

            def mlp_a0(si):
                b0 = sbs[si]
                xi = si % 2
                P.op("sp", lambda e: e.dma_start(out=XM[xi], in_=xs[b0 * 128:(b0 + 2) * 128, :].rearrange("(k p) n -> p k n", p=128)),
                     reads=[f"xs{b0}", f"xs{b0 + 1}"], writes=[f"xm{xi}"], dma_slot=f"xml{xi}")

            def mlp_a1(si):
                xi = si % 2
                for k in range(2):
                    sc = stat[:, 4 + k:5 + k]
                    rc = stat[:, 6 + k:7 + k]
                    P.op("act", lambda e, k=k, sc=sc: e.activation(out=H2[k], in_=XM[xi][:, k, :], func=AF.Square, accum_out=sc),
                         reads=[f"xm{xi}"], writes=[f"h2{k}", f"ss2{k}"])
                    P.op("act", lambda e, sc=sc, rc=rc: e.activation(out=rc, in_=sc, func=AF.Ln, scale=1.0 / D, bias=EPS),
                         reads=[f"ss2{k}"], writes=[f"rs2{k}"])
                    P.op("act", lambda e, rc=rc: e.activation(out=rc, in_=rc, func=AF.Exp, scale=-0.5),
                         reads=[f"rs2{k}"], writes=[f"rs2{k}"])
                    P.op("dve", lambda e, k=k, rc=rc: e.scalar_tensor_tensor(out=H2[k], in0=XM[xi][:, k, :], scalar=rc, in1=GB, op0=ALU.mult, op1=ALU.mult),
                         reads=[f"xm{xi}", f"rs2{k}", "gb"], writes=[f"h2{k}"])

            def mlp_a2(si):
                ti = si % 2
                for k in range(2):
                    for kc in range(8):
                        P.op("pe", lambda e, kc=kc, k=k: e.transpose(out=psT[:, kc, :], in_=H2[k][:, kc * 128:(kc + 1) * 128], identity=ident[:]),
                             reads=[f"h2{k}", "ident"], writes=["psT"])
                    P.op("dve", lambda e, k=k: e.tensor_copy(out=H2T[ti][:, :, k * 128:(k + 1) * 128], in_=psT[:]), reads=["psT"], writes=[f"h2t{ti}"])

            def mlp_up(si, hooks=None):
                for hc in range(32):
                    if hooks and hc in hooks:
                        hooks[hc]()
                    bank = hc % 3
                    U = pb[bank]
                    for kc in range(8):
                        P.op("pe", lambda e, kc=kc, U=U, hc=hc: e.matmul(U[:, 0:256], lhsT=WUP[:, kc, hc * 128:(hc + 1) * 128], rhs=H2T[si % 2][:, kc, :],
                                                                        start=(kc == 0), stop=(kc == 7)),
                             reads=[f"wup{kc}", f"h2t{si % 2}"], writes=[f"pb{bank}"])
                    ri = hc % 2
                    P.op("act", lambda e, U=U, ri=ri: e.activation(out=RTMP[ri], in_=U[:, 0:256], func=AF.Relu),
                         reads=[f"pb{bank}"], writes=[f"rtmp{ri}"])
                    P.op("pool", lambda e, ri=ri, hc=hc: e.tensor_tensor(out=HID[:, hc, :], in0=RTMP[ri], in1=RTMP[ri], op=ALU.mult),
                         reads=[f"rtmp{ri}"], writes=[f"hid{hc}"])

            def mlp_down(si):
                b0 = sbs[si]
                xi = si % 2
                for k in range(2):
                    for hf in range(2):
                        bank = 3 + k * 2 + hf
                        Dp = pb[bank]
                        for hc in range(32):
                            P.op("pe", lambda e, hc=hc, Dp=Dp, k=k, hf=hf: e.matmul(Dp[:], lhsT=HID[:, hc, k * 128:(k + 1) * 128], rhs=WDN[:, hc, hf * 512:(hf + 1) * 512],
                                                                                  start=(hc == 0), stop=(hc == 31)),
                                 reads=[f"hid{hc}", f"wdn{hc}"], writes=[f"pb{bank}"])
                        xsl = XM[xi][:, k, hf * 512:(hf + 1) * 512]
                        P.op("dve", lambda e, Dp=Dp, xsl=xsl: e.tensor_tensor(out=xsl, in0=Dp[:], in1=xsl, op=ALU.add),
                             reads=[f"pb{bank}", f"xm{xi}"], writes=[f"xm{xi}"])
                if not last:
                    P.op("pool", lambda e: e.dma_start(out=xs[b0 * 128:(b0 + 2) * 128, :].rearrange("(k p) n -> p k n", p=128), in_=XM[xi]),
                         reads=[f"xm{xi}"], writes=[f"xs{b0}", f"xs{b0 + 1}"], dma_slot=f"xms{xi}")
                else:
                    for k in range(2):
                        sc = stat[:, 4 + k:5 + k]
                        rc = stat[:, 6 + k:7 + k]
                        sc = stat[:, k:k + 1]
                        rc = stat[:, 2 + k:3 + k]
                        P.op("act", lambda e, k=k, sc=sc: e.activation(out=H2[k], in_=XM[xi][:, k, :], func=AF.Square, accum_out=sc),
                             reads=[f"xm{xi}"], writes=[f"h2{k}", f"fss{k}"])
                        P.op("act", lambda e, sc=sc, rc=rc: e.activation(out=rc, in_=sc, func=AF.Ln, scale=1.0 / D, bias=EPS),
                             reads=[f"fss{k}"], writes=[f"frs{k}"])
                        P.op("act", lambda e, rc=rc: e.activation(out=rc, in_=rc, func=AF.Exp, scale=-0.5),
                             reads=[f"frs{k}"], writes=[f"frs{k}"])
                        P.op("dve", lambda e, k=k, rc=rc: e.scalar_tensor_tensor(out=XM[xi][:, k, :], in0=XM[xi][:, k, :], scalar=rc, in1=GFIN,
                                                                                 op0=ALU.mult, op1=ALU.mult),
                             reads=[f"xm{xi}", f"frs{k}", "gfin"], writes=[f"xm{xi}"])
                    yb = b0 - HALO
                    P.op("pool", lambda e: e.dma_start(out=y[yb * 128:(yb + 2) * 128, :].rearrange("(k p) n -> p k n", p=128), in_=XM[xi]),
                         reads=[f"xm{xi}"], writes=[f"y{yb}"], dma_slot=f"xms{xi}")

            n = len(sbs)
            mlp_a0(0)
            mlp_a1(0)
            mlp_a2(0)
            for si in range(n):
                hooks = None
                if si + 1 < n:
                    hooks = {0: (lambda si=si: mlp_a0(si + 1)), 10: (lambda si=si: mlp_a1(si + 1)), 22: (lambda si=si: mlp_a2(si + 1))}
                mlp_up(si, hooks)
                if si == n - 1 and not last:
                    for k2 in range(4):
                        P.op("pool", lambda e, k2=k2: e.dma_start(out=WIN[:, 2 * k2:2 * k2 + 2, :],
                                                                  in_=w_in[l + 1, k2 * 256:(k2 + 1) * 256, :].rearrange("(k p) n -> p k n", p=128)),
                             writes=[f"wup{k}" for k in range(8)], dma_slot="wl0")
                    pre_win[0] = True
                mlp_down(si)

        for l in range(nlayers):
            phase1(l)
            P.barrier()
            phase2(l, last=(l == nlayers - 1))
            P.barrier(final=(l == nlayers - 1))

        names = P.finalize()
        sems = {}
        for k in names:
            nm = "s_" + "_".join(str(t) for t in k)
            sems[k] = es.enter_context(nc.semaphore(nm))
        with nc.Block() as block:
            @block.sync
            def _(e):
                P.run("sp", e, sems)

            @block.tensor
            def _(e):
                P.run("pe", e, sems)

            @block.scalar
            def _(e):
                P.run("act", e, sems)

            @block.vector
            def _(e):
                P.run("dve", e, sems)

            @block.gpsimd
            def _(e):
                P.run("pool", e, sems)
    return nc


def _ubias(rpb_l):
    a = np.arange(2)[:, None, None, None]
    kc = np.arange(64)[None, :, None, None]
    bq = np.arange(2)[None, None, :, None]
    c = np.arange(64)[None, None, None, :]
    cs = np.clip(c - 8, 0, 48)
    colvalid = (kc >= cs) & (kc <= cs + 15)
    dci = np.clip(kc - c + 15, 0, 30) + 0 * a + 0 * bq
    tabs = []
    specs = [(-6, False), (-4, False), (-2, False), (0, False), (2, False), (4, False), (6, False), (-4, True), (4, True)]
    for delta, normal in specs:
        dr = delta + a - bq
        valid = colvalid & (np.abs(dr) <= 7)
        if normal:
            valid = valid & (dr >= -4) & (dr <= 3)
        dri = np.clip(dr + 7, 0, 14) + 0 * kc + 0 * c
        vals = rpb_l[:, dri, dci]
        t = np.where(np.broadcast_to(valid, vals.shape[1:])[None], vals, np.float32(NEGV)).astype(np.float32)
        tabs.append(t.reshape(8, 128, 128))
    U = np.stack(tabs, 1)
    return np.ascontiguousarray(U.transpose(2, 1, 0, 3)).reshape(128, 9, 8 * 128)


def _rope_tables(pos):
    half = 32
    inv = (np.float32(10000.0) ** (-(np.arange(half, dtype=np.float32) / np.float32(half)))).astype(np.float32)
    ang = pos.astype(np.float32)[:, None] * inv[None, :]
    cos = np.cos(ang).astype(np.float32)
    sin = np.sin(ang).astype(np.float32)
    cc = np.concatenate([cos, cos], 1)
    ss = np.concatenate([-sin, sin], 1)
    q = np.float32(0.125)
    return np.stack([cc * q, ss * q, cc, ss], 1).astype(np.float32)


def _consts():
    k = np.arange(128)[:, None]
    q = np.arange(128)[None, :]
    prev = np.where(k >= q, 0.0, NEGV).astype(np.float32)
    nxt = np.where(k <= q, 0.0, NEGV).astype(np.float32)
    neg = np.full((128, 512), NEGV, np.float32)
    cm = np.concatenate([np.tile(prev, (1, 4)), np.tile(nxt, (1, 4)), neg], 1)
    return np.ascontiguousarray(cm), np.eye(128, dtype=np.float32)


_NC_CACHE = {}


def kernel(x_prompt, x_sample, norm_mix, w_in, rpb, sink, w_out, norm_mlp, w_up, w_down, norm_final):
    f32 = np.float32
    xa = np.concatenate([np.asarray(x_prompt, f32).reshape(-1, D), np.asarray(x_sample, f32).reshape(-1, D)], 0)
    xpad = np.zeros((T_ALL + 2 * HALO * 128, D), f32)
    xpad[HALO * 128:HALO * 128 + T_ALL] = xa
    w_in = np.array(np.asarray(w_in, f32))
    qsw = w_in[:, :, 1536:2048].reshape(NLAYERS, D, 8, 64)
    w_in[:, :, 1536:2048] = qsw[:, :, [0, 4, 1, 5, 2, 6, 3, 7], :].reshape(NLAYERS, D, 512)
    w_out = np.ascontiguousarray(np.asarray(w_out, f32))
    w_up = np.ascontiguousarray(np.asarray(w_up, f32))
    w_down = np.ascontiguousarray(np.asarray(w_down, f32))
    gmix = np.ascontiguousarray(np.asarray(norm_mix, f32))
    gmlp = np.ascontiguousarray(np.asarray(norm_mlp, f32))
    gfin = np.ascontiguousarray(np.asarray(norm_final, f32))
    rpb = np.asarray(rpb, f32)
    ubias = np.stack([_ubias(rpb[l]) for l in range(NLAYERS)], 0)
    sink16 = np.full((NLAYERS, 16), -1e4, f32)
    sink16[:, 8:] = np.asarray(sink, f32)
    sink16 = sink16.reshape(-1)
    cmask, ident = _consts()
    g = np.arange(-HALO * 128, T_ALL + HALO * 128)
    pos = np.zeros_like(g)
    ends = list(SEQ_STARTS[1:]) + [T_ALL]
    for s0, s1 in zip(SEQ_STARTS, ends):
        m = (g >= s0) & (g < s1)
        pos[m] = g[m] - s0
    rope_all = _rope_tables(pos)
    bset = set(SEQ_STARTS) | {T_ALL}
    in_maps = []
    for c in range(8):
        g0 = 6144 * c
        seli = np.zeros((128, 8, 128), f32)
        for p, B in enumerate(BOUNDS):
            isb = (g0 + (B - HALO) * 128) in bset
            seli[:, 2 * p + (1 if isb else 0), :] = np.eye(128, dtype=f32)
        in_maps.append({
            "x_ext": xpad[g0:g0 + NB * 128],
            "w_in": w_in, "w_out": w_out, "w_up": w_up, "w_down": w_down,
            "gmix": gmix, "gmlp": gmlp, "gfin": gfin, "ubias": ubias, "sink16": sink16,
            "rope": np.ascontiguousarray(rope_all[g0:g0 + NB * 128].reshape(NB, 128, 256)),
            "cmask": cmask, "seli": seli.reshape(128, 1024), "ident": ident,
        })
    nl = int(os.environ.get("KNL", NLAYERS))
    if nl not in _NC_CACHE:
        _NC_CACHE[nl] = build_program(nl)
    res = run_bass_kernel_spmd(_NC_CACHE[nl], in_maps, core_ids=list(range(8)))
    yall = np.concatenate([np.asarray(r["y"], f32) for r in res.results], 0)
    y_prompt = yall[:16384].reshape(2, 8192, D)
    y_sample = yall[16384:].reshape(2, 16384, D)
    return (y_prompt, y_sample)
```

```python
import os
import numpy as np
from contextlib import ExitStack
import concourse.bass as bass
import concourse.mybir as mybir
from concourse.bass_utils import run_bass_kernel_spmd

F32 = mybir.dt.float32
BF16 = mybir.dt.bfloat16
AF = mybir.ActivationFunctionType
ALU = mybir.AluOpType

D = 1024
NB = 64
HALO = 8
CORE_NB = 48
NEGV = -30000.0
BOUNDS = (8, 24, 40, 56)
SEQ_STARTS = (0, 8192, 16384, 32768)
T_ALL = 49152
EPS = 1e-5
NLAYERS = 4

QUEUES = ("pe", "act", "dve", "pool", "sp")
SAME_ENGINE_SYNC = {"act", "dve", "pool"}
EPOCH = 20000


class Prog:
    def __init__(self):
        self.ops = {q: [] for q in QUEUES}
        self.state = {}
        self.waited = {q: {} for q in QUEUES}
        self.dma_count = {}
        self.psum = set()

    def _deps(self, reads, writes):
        deps = {}
        for k in reads:
            st = self.state.get(k)
            if st:
                for s, i in st["w"].items():
                    if deps.get(s, -1) < i:
                        deps[s] = i
        for k in writes:
            st = self.state.get(k)
            if st:
                for s, i in st["w"].items():
                    if deps.get(s, -1) < i:
                        deps[s] = i
                for s, i in st["r"].items():
                    if deps.get(s, -1) < i:
                        deps[s] = i
        return deps

    def op(self, q, fn, reads=(), writes=(), dma_slot=None):
        pk = self.psum
        if q != "pe" and any(k in pk for k in reads):
            writes = list(writes) + [k for k in reads if k in pk]
            reads = [k for k in reads if k not in pk]
        deps = self._deps(reads, writes)
        waits = []
        wd = self.waited[q]
        for s, i in deps.items():
            if s == q and q not in SAME_ENGINE_SYNC:
                continue
            if wd.get(s, -1) >= i:
                continue
            wd[s] = i
            waits.append((s, i))
        idx = len(self.ops[q])
        rec = {"fn": fn, "waits": waits, "inc": False, "dma": dma_slot}
        if dma_slot is not None:
            n = self.dma_count.get(dma_slot, 0)
            self.dma_count[dma_slot] = n + 1
            tok = (("dma", dma_slot), n)
        else:
            tok = (q, idx)
        self.ops[q].append(rec)
        s, i = tok
        for k in reads:
            st = self.state.setdefault(k, {"w": {}, "r": {}})
            if st["r"].get(s, -1) < i:
                st["r"][s] = i
        for k in writes:
            self.state[k] = {"w": {s: i}, "r": {}}
        return tok

    def barrier(self, final=False):
        toks = []
        for q in QUEUES:
            for i in range(len(self.ops[q]) - 1, -1, -1):
                r = self.ops[q][i]
                if r["fn"] is not None and r["dma"] is None:
                    toks.append((q, i))
                    break
        for k, n in self.dma_count.items():
            toks.append((("dma", k), n - 1))
        for q in (("sp",) if final else QUEUES):
            waits = []
            wd = self.waited[q]
            for s, i in toks:
                if s == q:
                    continue
                if wd.get(s, -1) >= i:
                    continue
                wd[s] = i
                waits.append((s, i))
            self.ops[q].append({"fn": None, "waits": waits, "inc": False, "dma": None})
        self.state = {}

    def finalize(self):
        for q in QUEUES:
            for rec in self.ops[q]:
                for s, i in rec["waits"]:
                    if not isinstance(s, tuple):
                        self.ops[s][i]["inc"] = True
        self.val = {}
        for q in QUEUES:
            c = 0
            for i, rec in enumerate(self.ops[q]):
                if rec["inc"] and rec["dma"] is None:
                    c += 1
                    self.val[(q, i)] = c
        names = []
        for q in QUEUES:
            c = sum(1 for r in self.ops[q] if r["inc"] and r["dma"] is None)
            for e in range(c // EPOCH + 1):
                names.append((q, e))
        for k in self.dma_count:
            names.append(("dma", k))
        return names

    def run(self, q, eng, sems):
        def sv(tok):
            s, i = tok
            if isinstance(s, tuple):
                return sems[("dma", s[1])], 16 * (i + 1)
            v = self.val[(s, i)]
            e = (v - 1) // EPOCH
            return sems[(s, e)], v - e * EPOCH
        for i, rec in enumerate(self.ops[q]):
            for tok in rec["waits"]:
                sem, v = sv(tok)
                eng.wait_ge(sem, v)
            if rec["fn"] is None:
                continue
            ins = rec["fn"](eng)
            if rec["dma"] is not None:
                ins.then_inc(sems[("dma", rec["dma"])], 16)
            elif rec["inc"]:
                sem, v = sv((q, i))
                ins.then_inc(sem, 1)


NEGT = "neg"
NORMAL_WIN = [(-2, 7, 7), (-1, 2, 2), (0, 3, 3), (1, 4, 4), (2, 8, 8)]


def na_window(b):
    for p, B in enumerate(BOUNDS):
        if b == B:
            return [(b - 2, 7, NEGT), (b - 1, 2, NEGT), (b, 3, 3), (b + 1, 4, 4), (b + 2, 8, 5), (b + 3, NEGT, 6)], p
        if b == B + 1:
            return [(b - 2, 7, NEGT), (b - 1, 2, 2), (b, 3, 3), (b + 1, 4, 4), (b + 2, 8, 5)], p
        if b == B - 2:
            return [(b - 2, 7, 1), (b - 1, 2, 2), (b, 3, 3), (b + 1, 4, 4), (b + 2, 8, NEGT)], p
        if b == B - 1:
            return [(b - 3, NEGT, 0), (b - 2, 7, 1), (b - 1, 2, 2), (b, 3, 3), (b + 1, 4, NEGT), (b + 2, 8, NEGT)], p
    return [(b + o, tn, tc) for o, tn, tc in NORMAL_WIN], None


def build_program(nlayers=NLAYERS, debug_out=None):
    nc = bass.Bass("TRN2", target_bir_lowering=False)

    def din(name, shape):
        return nc.dram_tensor(name, shape, F32, kind="ExternalInput").ap()

    x_ext = din("x_ext", [NB * 128, D])
    w_in = din("w_in", [NLAYERS, D, 2304])
    w_out = din("w_out", [NLAYERS, D, D])
    w_up = din("w_up", [NLAYERS, D, 4096])
    w_down = din("w_down", [NLAYERS, 4096, D])
    gmix_d = din("gmix", [NLAYERS, D])
    gmlp_d = din("gmlp", [NLAYERS, D])
    gfin_d = din("gfin", [D])
    ubias_d = din("ubias", [NLAYERS, 128, 9, 8 * 128])
    sink_d = din("sink16", [NLAYERS * 16])
    rope_d = din("rope", [NB, 128, 256])
    cmask_d = din("cmask", [128, 3 * 512])
    seli_d = din("seli", [128, 8 * 128])
    ident_d = din("ident", [128, 128])
    y = nc.dram_tensor("y", [CORE_NB * 128, D], F32, kind="ExternalOutput").ap()
    xs = nc.dram_tensor("xs", [NB * 128, D], F32).ap()

    P = Prog()
    es = ExitStack()
    with es:
        def sb(name, shape, dt):
            return es.enter_context(nc.sbuf_tensor(name, shape, dt))

        def ps(name, shape, dt):
            return es.enter_context(nc.psum_tensor(name, shape, dt))

        ABF = 79872
        AF32 = 9344
        abf = sb("abf_sb", [128, ABF], BF16)
        af32 = sb("af32_sb", [128, AF32], F32)
        ident = sb("ident_sb", [128, 128], BF16)
        seli = sb("seli_sb", [128, 8, 128], BF16)
        cmask = sb("cmask_sb", [128, 3, 512], BF16)
        esink = sb("esink_sb", [128, NLAYERS * 16], F32)
        stat = sb("stat_sb", [128, 8], F32)
        den = sb("den_sb", [128, 16], F32)

        psT = ps("psT", [128, 8, 128], BF16)
        pb = [ps(f"pb{i}", [128, 512], F32) for i in range(7)]
        P.psum = {"psT"} | {f"pb{i}" for i in range(7)}

        def bfv(off, n):
            return abf[:, off:off + n]

        def f32v(off, n):
            return af32[:, off:off + n]

        o = 0
        WIN = bfv(o, 8 * 2304).rearrange("p (k n) -> p k n", n=2304); o += 8 * 2304
        WOUT = bfv(o, 8 * 1024).rearrange("p (k n) -> p k n", n=1024); o += 8 * 1024
        UT = bfv(o, 9 * 8 * 128).rearrange("p (t h q) -> p t h q", h=8, q=128); o += 9 * 8 * 128
        KR = []
        for s_ in range(8):
            KR.append(bfv(o, 5 * 128).rearrange("p (c t) -> p c t", t=128)); o += 5 * 128
        QR_OFF = o
        NQR = 6
        QBD, QSA, QSB = [], [], []
        for s_ in range(NQR):
            QBD.append(bfv(o, 1024).rearrange("p (i t) -> p i t", t=256)); o += 1024
            QSA.append(bfv(o, 512).rearrange("p (a t) -> p a t", t=128)); o += 512
            QSB.append(bfv(o, 512).rearrange("p (a t) -> p a t", t=128)); o += 512
        QR_ALL = bfv(QR_OFF, NQR * 2048)
        VR = []
        for s_ in range(8):
            VR.append(bfv(o, 10 * 65).rearrange("p (h d) -> p h d", d=65)); o += 10 * 65
        HB = [bfv(o + i * 1024, 1024) for i in range(2)]; o += 2048
        HT = [bfv(o + i * 1024, 1024).rearrange("p (k t) -> p k t", t=128) for i in range(2)]; o += 2048
        QKTOK = [bfv(o + i * 1664, 1664) for i in range(2)]; o += 2 * 1664
        PT = bfv(o, 12 * 512).rearrange("p (c t) -> p c t", t=512); o += 12 * 512
        PTS = bfv(o, 6 * 512).rearrange("p (c t) -> p c t", t=512); o += 6 * 512
        ATT = [bfv(o + i * 1024, 1024) for i in range(2)]; o += 2048
        ATT_T = [bfv(o + i * 1024, 1024).rearrange("p (k t) -> p k t", t=128) for i in range(2)]; o += 2048
        assert o <= ABF, o
        o = 0
        WUP = bfv(o, 8 * 4096).rearrange("p (k n) -> p k n", n=4096); o += 8 * 4096
        WDN = bfv(o, 32 * 1024).rearrange("p (k n) -> p k n", n=1024); o += 32 * 1024
        HID = bfv(o, 32 * 256).rearrange("p (k t) -> p k t", t=256); o += 32 * 256
        H2 = [bfv(o + i * 1024, 1024) for i in range(2)]; o += 2048
        H2T = [bfv(o + i * 2048, 2048).rearrange("p (k t) -> p k t", t=256) for i in range(2)]; o += 4096
        assert o <= ABF, o
        o = 0
        STG = [f32v(o + i * 1152, 1152) for i in range(3)]; o += 3 * 1152
        o1 = o
        XP = [f32v(o + i * 1024, 1024) for i in range(2)]; o += 2048
        RP = [f32v(o + i * 256, 256).rearrange("p (a d) -> p a d", d=64) for i in range(3)]; o += 768
        RTA = f32v(o, 512).rearrange("p (h d) -> p h d", d=64); o += 512
        RTB = f32v(o, 512).rearrange("p (h d) -> p h d", d=64); o += 512
        XO = [f32v(o + i * 1024, 1024) for i in range(2)]; o += 2048
        assert o <= AF32, o
        o = o1
        XM = [f32v(o + i * 2048, 2048).rearrange("p (k n) -> p k n", n=1024) for i in range(2)]; o += 4096
        RTMP = [f32v(o + i * 256, 256) for i in range(2)]; o += 512
        GFIN = f32v(o, 1024); o += 1024
        assert o <= AF32, o

        GB = f32v(0, 1024)
        stg_ctr = [0]

        def wload(dmas, keys, slot="wl"):
            tok = None
            for out_ap, in_ap in dmas:
                tok = P.op("pool", lambda e, out_ap=out_ap, in_ap=in_ap: e.dma_start(out=out_ap, in_=in_ap), dma_slot=slot)
            for k in keys:
                P.state[k] = {"w": {tok[0]: tok[1]}, "r": {}}

        def load_cast(src_ap, n, dst_ap, dst_key, scale_ap=None, scale_key=None):
            i = stg_ctr[0]
            stg_ctr[0] += 1
            slot = i % 3
            st = STG[slot][:, :n]
            P.op("sp", lambda e: e.dma_start(out=st, in_=src_ap), writes=[f"stg{slot}"], dma_slot=f"stg{slot}")
            eng = ("pool", "act", "dve")[i % 3]
            rd = [f"stg{slot}"] + ([scale_key] if scale_key else [])
            if eng == "act":
                if scale_ap is None:
                    P.op("act", lambda e: e.copy(out=dst_ap, in_=st), reads=rd, writes=[dst_key])
                else:
                    P.op("act", lambda e: e.activation(out=dst_ap, in_=st, func=AF.Copy, scale=scale_ap), reads=rd, writes=[dst_key])
            else:
                if scale_ap is None:
                    P.op(eng, lambda e: e.tensor_copy(out=dst_ap, in_=st), reads=rd, writes=[dst_key])
                else:
                    P.op(eng, lambda e: e.tensor_scalar(out=dst_ap, in0=st, scalar1=scale_ap, scalar2=None, op0=ALU.mult), reads=rd, writes=[dst_key])

        load_cast(ident_d, 128, ident[:], "ident")
        load_cast(seli_d, 1024, seli[:].rearrange("p a t -> p (a t)"), "seli")
        for j in range(3):
            load_cast(cmask_d[:, j * 512:(j + 1) * 512], 512, cmask[:, j, :], f"cmask{j}")
        P.op("sp", lambda e: e.dma_start(out=esink[:], in_=sink_d.partition_broadcast(128)), writes=["esink"], dma_slot="c2")
        P.op("act", lambda e: e.activation(out=esink[:], in_=esink[:], func=AF.Exp), reads=["esink"], writes=["esink"])
        NEG512 = cmask[:, 2, :]
        NEG128 = cmask[:, 2, 0:128]

        def srckey(l, b):
            return f"xin{b}" if l == 0 else f"xs{b}"

        shift = NLAYERS - nlayers
        pre_win = [False]
        pre_wup = [False]

        def phase1(l):
            lo, hi = 2 + 2 * (l + shift), 62 - 2 * (l + shift)
            klo, khi = lo - 2, hi + 2
            src = x_ext if l == 0 else xs
            dm = []
            for k2 in range(4):
                dm.append((WIN[:, 2 * k2:2 * k2 + 2, :], w_in[l, k2 * 256:(k2 + 1) * 256, :].rearrange("(k p) n -> p k n", p=128)))
            for k4 in range(2):
                dm.append((WOUT[:, 4 * k4:4 * k4 + 4, :], w_out[l, k4 * 512:(k4 + 1) * 512, :].rearrange("(k p) n -> p k n", p=128)))
            if pre_win[0]:
                for kc in range(8):
                    P.state[f"win{kc}"] = {"w": {("dma", "wl0"): P.dma_count["wl0"] - 1}, "r": {}}
                pre_win[0] = False
            else:
                wload(dm[:4], [f"win{kc}" for kc in range(8)], "wl0")
            wload(dm[4:] + [(UT[:].rearrange("p t h q -> p (t h q)"), ubias_d[l].rearrange("p t n -> p (t n)"))],
                  [f"wout{kc}" for kc in range(8)] + [f"ut{t}" for t in range(9)], "wl1")
            P.op("sp", lambda e: e.dma_start(out=GB, in_=gmix_d[l].partition_broadcast(128)), writes=["gb", "stg0"], dma_slot="c0")
            P.op("pool", lambda e: e.memset(QR_ALL, 0.0), writes=[f"qr{i}{x}" for i in range(NQR) for x in "abcd"])
            for s in range(8):
                P.op("pool", lambda e, s=s: e.memset(VR[s][:, :, 64:65], 1.0), writes=[f"vr1_{s}"])

            def proj_a(p):
                xi, hi_, ri = p % 2, p % 2, p % 3
                P.op("sp", lambda e: e.dma_start(out=XP[xi], in_=src[p * 128:(p + 1) * 128, :]),
                     reads=[srckey(l, p)], writes=[f"xp{xi}"], dma_slot=f"xp{xi}")
                P.op("sp", lambda e: e.dma_start(out=RP[ri].rearrange("p a d -> p (a d)"), in_=rope_d[p]),
                     writes=[f"rp{ri}"], dma_slot=f"rp{ri}")
                sc = stat[:, xi:xi + 1]
                rc = stat[:, 2 + xi:3 + xi]
                P.op("act", lambda e: e.activation(out=HB[hi_], in_=XP[xi], func=AF.Square, accum_out=sc),
                     reads=[f"xp{xi}"], writes=[f"hb{hi_}", f"ss{xi}"])
                P.op("act", lambda e: e.activation(out=rc, in_=sc, func=AF.Ln, scale=1.0 / D, bias=EPS),
                     reads=[f"ss{xi}"], writes=[f"rs{xi}"])
                P.op("act", lambda e: e.activation(out=rc, in_=rc, func=AF.Exp, scale=-0.5),
                     reads=[f"rs{xi}"], writes=[f"rs{xi}"])
                P.op("dve", lambda e: e.scalar_tensor_tensor(out=HB[hi_], in0=XP[xi], scalar=rc, in1=GB, op0=ALU.mult, op1=ALU.mult),
                     reads=[f"xp{xi}", f"rs{xi}", "gb"], writes=[f"hb{hi_}"])

            SCR = [(pb[2][:], pb[2][:].bitcast(BF16).rearrange("p (c t) -> p c t", t=128), "pb2"),
                   (pb[3][:], pb[3][:].bitcast(BF16).rearrange("p (c t) -> p c t", t=128), "pb3"),
                   (psT[:].rearrange("p c t -> p (c t)").bitcast(F32), psT[:], "psT")]

            def scr():
                i = sbank[0] % 2
                sbank[0] += 1
                return SCR[i]

            def tscr():
                return SCR[2]

            def proj_b(p):
                hi_ = p % 2
                _, tv, tk = tscr()
                for kc in range(8):
                    P.op("pe", lambda e, kc=kc: e.transpose(out=tv[:, kc, :], in_=HB[hi_][:, kc * 128:(kc + 1) * 128], identity=ident[:]),
                         reads=[f"hb{hi_}", "ident"], writes=[tk])
                P.op("dve", lambda e: e.tensor_copy(out=HT[hi_], in_=tv), reads=[tk], writes=[f"ht{hi_}"])

            def proj_c(p, early_d1=False, only=None):
                hi_, ri, slot, qi = p % 2, p % 3, p % 8, p % 2
                qk = QKTOK[qi]
                groups = [(0, 512), (512, 512), (1024, 512), (1536, 512), (2048, 256)]
                for gi, (c0, n) in enumerate(groups):
                    if only is not None and gi != only:
                        continue
                    bank = gi % 2
                    pp = pb[bank]
                    for kc in range(8):
                        P.op("pe", lambda e, kc=kc, pp=pp, c0=c0, n=n: e.matmul(pp[:, :n], lhsT=HT[hi_][:, kc, :], rhs=WIN[:, kc, c0:c0 + n],
                                                                              start=(kc == 0), stop=(kc == 7)),
                             reads=[f"ht{hi_}", f"win{kc}"], writes=[f"pb{bank}"])
                    if gi == 0:
                        P.op("dve", lambda e, pp=pp: e.tensor_scalar(out=qk[:, 0:512], in0=pp[:], scalar1=0.125, scalar2=None, op0=ALU.mult),
                             reads=[f"pb{bank}"], writes=[f"qk{qi}a"])
                    elif gi == 1:
                        P.op("dve", lambda e, pp=pp: e.tensor_copy(out=qk[:, 512:1024], in_=pp[:]),
                             reads=[f"pb{bank}"], writes=[f"qk{qi}b"])
                    elif gi == 2:
                        P.op("act", lambda e, pp=pp: e.copy(out=VR[slot][:, 0:8, 0:64], in_=pp[:].rearrange("p (h d) -> p h d", d=64)),
                             reads=[f"pb{bank}"], writes=[f"vra{slot}"])
                        if early_d1:
                            proj_d1(p)
                    elif gi == 3:
                        xv = pp[:].rearrange("p (h d) -> p h d", d=64)
                        cc = RP[ri][:, 0:1, :].to_broadcast([128, 8, 64])
                        ss1 = RP[ri][:, 1:2, 0:32].to_broadcast([128, 8, 32])
                        ss2 = RP[ri][:, 1:2, 32:64].to_broadcast([128, 8, 32])
                        P.op("dve", lambda e, xv=xv, cc=cc: e.tensor_tensor(out=RTA[:], in0=xv, in1=cc, op=ALU.mult),
                             reads=[f"pb{bank}", f"rp{ri}"], writes=["rta"])
                        P.op("dve", lambda e, xv=xv, ss1=ss1: e.tensor_tensor(out=RTB[:, :, 0:32], in0=xv[:, :, 32:64], in1=ss1, op=ALU.mult),
                             reads=[f"pb{bank}", f"rp{ri}"], writes=["rtb1"])
                        P.op("dve", lambda e, xv=xv, ss2=ss2: e.tensor_tensor(out=RTB[:, :, 32:64], in0=xv[:, :, 0:32], in1=ss2, op=ALU.mult),
                             reads=[f"pb{bank}", f"rp{ri}"], writes=["rtb2"])
                        P.op("pool", lambda e: e.tensor_tensor(out=qk[:, 1024:1536].rearrange("p (h d) -> p h d", d=64), in0=RTA[:], in1=RTB[:], op=ALU.add),
                             reads=["rta", "rtb1", "rtb2"], writes=[f"qk{qi}c"])
                    else:
                        P.op("act", lambda e, pp=pp: e.copy(out=VR[slot][:, 8:10, 0:64], in_=pp[:, 128:256].rearrange("p (h d) -> p h d", d=64)),
                             reads=[f"pb{bank}"], writes=[f"vrs{slot}"])
                        xv = pp[:, 0:128].rearrange("p (h d) -> p h d", d=64)
                        cc = RP[ri][:, 2:3, :].to_broadcast([128, 2, 64])
                        ss1 = RP[ri][:, 3:4, 0:32].to_broadcast([128, 2, 32])
                        ss2 = RP[ri][:, 3:4, 32:64].to_broadcast([128, 2, 32])
                        P.op("dve", lambda e, xv=xv, cc=cc: e.tensor_tensor(out=RTA[:, 0:2, :], in0=xv, in1=cc, op=ALU.mult),
                             reads=[f"pb{bank}", f"rp{ri}"], writes=["rta"])
                        P.op("dve", lambda e, xv=xv, ss1=ss1: e.tensor_tensor(out=RTB[:, 0:2, 0:32], in0=xv[:, :, 32:64], in1=ss1, op=ALU.mult),
                             reads=[f"pb{bank}", f"rp{ri}"], writes=["rtb1"])
                        P.op("dve", lambda e, xv=xv, ss2=ss2: e.tensor_tensor(out=RTB[:, 0:2, 32:64], in0=xv[:, :, 0:32], in1=ss2, op=ALU.mult),
                             reads=[f"pb{bank}", f"rp{ri}"], writes=["rtb2"])
                        P.op("pool", lambda e: e.tensor_tensor(out=qk[:, 1536:1664].rearrange("p (h d) -> p h d", d=64), in0=RTA[:, 0:2, :], in1=RTB[:, 0:2, :], op=ALU.add),
                             reads=["rta", "rtb1", "rtb2"], writes=[f"qk{qi}d"])

            def proj_d1(p):
                slot, qi, qs = p % 8, p % 2, p % NQR
                qk = QKTOK[qi]
                _, tv, tk = tscr()
                for c in range(8):
                    P.op("pe", lambda e, c=c: e.transpose(out=tv[:, c, :], in_=qk[:, c * 128:(c + 1) * 128], identity=ident[:]),
                         reads=[f"qk{qi}a", f"qk{qi}b", "ident"], writes=[tk])
                P.op("dve", lambda e: e.tensor_copy(out=QBD[qs][0:64, :, 0:128], in_=tv[0:64, 0:4, :]), reads=[tk], writes=[f"qr{qs}a"])
                P.op("dve", lambda e: e.tensor_copy(out=QBD[qs][64:128, :, 128:256], in_=tv[64:128, 0:4, :]), reads=[tk], writes=[f"qr{qs}b"])
                P.op("dve", lambda e: e.tensor_copy(out=KR[slot][:, 0:4, :], in_=tv[:, 4:8, :]), reads=[tk], writes=[f"kr{slot}a"])

            def proj_d2(p):
                slot, qi, qs = p % 8, p % 2, p % NQR
                qk = QKTOK[qi]
                _, pT2, tk = scr()
                for a in range(4):
                    P.op("pe", lambda e, a=a: e.transpose(out=pT2[:, a, :], in_=qk[:, 1024 + a * 128:1024 + (a + 1) * 128], identity=ident[:]),
                         reads=[f"qk{qi}c", "ident"], writes=[tk])
                P.op("pe", lambda e: e.transpose(out=pT2[:, 4, :], in_=qk[:, 1536:1664], identity=ident[:]),
                     reads=[f"qk{qi}d", "ident"], writes=[tk])
                P.op("act", lambda e: e.copy(out=QSA[qs][0:64, :, :], in_=pT2[0:64, 0:4, :]), reads=[tk], writes=[f"qr{qs}c"])
                P.op("act", lambda e: e.copy(out=QSB[qs][64:128, :, :], in_=pT2[64:128, 0:4, :]), reads=[tk], writes=[f"qr{qs}d"])
                P.op("act", lambda e: e.copy(out=KR[slot][:, 4, :], in_=pT2[:, 4, :]), reads=[tk], writes=[f"kr{slot}b"])

            def proj_d(p):
                proj_d1(p)
                proj_d2(p)

            sbank = [0]

            ostart = [set()]

            def pv(oo_bank, oo, lhsT, rhs, rkeys):
                st = oo_bank not in ostart[0]
                ostart[0].add(oo_bank)
                P.op("pe", lambda e: e.matmul(oo, lhsT=lhsT, rhs=rhs, start=st, stop=True, skip_group_check=True),
                     reads=rkeys, writes=[f"pb{oo_bank}"])

            def attn_na(b, mid=None):
                ostart[0] = set()
                win, pidx = na_window(b)
                nch = len(win)
                qs = b % NQR
                groups = [(j, qd) for j in range(nch) for qd in range(2)]

                def s_group(gi):
                    j, qd = groups[gi]
                    kb, tn, tcl = win[j]
                    ks = kb % 8
                    S, _, bkey = scr()
                    for pi in range(2):
                        i = 2 * qd + pi
                        P.op("pe", lambda e, S=S, pi=pi, i=i, ks=ks: e.matmul(S[:, pi * 256:(pi + 1) * 256], lhsT=KR[ks][:, i, :], rhs=QBD[qs][:, i, :],
                                                                             start=(pi == 0), stop=False),
                             reads=[f"kr{ks}a", f"qr{qs}a", f"qr{qs}b"], writes=[bkey])

                    def tab(t):
                        return NEG512 if t == NEGT else UT[:, t, 4 * qd:4 * qd + 4, :].rearrange("p h q -> p (h q)")

                    def tkey(t):
                        return "cmask2" if t == NEGT else f"ut{t}"
                    if tn == tcl:
                        P.op("pe", lambda e, S=S, r=tab(tn): e.matmul(S[:], lhsT=ident[:], rhs=r, start=False, stop=True),
                             reads=["ident", tkey(tn)], writes=[bkey])
                    else:
                        P.op("pe", lambda e, S=S, r=tab(tn): e.matmul(S[:], lhsT=seli[:, 2 * pidx, :], rhs=r, start=False, stop=False),
                             reads=["seli", tkey(tn)], writes=[bkey])
                        P.op("pe", lambda e, S=S, r=tab(tcl): e.matmul(S[:], lhsT=seli[:, 2 * pidx + 1, :], rhs=r, start=False, stop=True),
                             reads=["seli", tkey(tcl)], writes=[bkey])
                    P.op("act", lambda e, S=S, gi=gi: e.activation(out=PT[:, gi, :], in_=S[:], func=AF.Exp),
                         reads=[bkey], writes=[f"pt{gi}"])

                def pv_group(gi):
                    j, qd = groups[gi]
                    ks = win[j][0] % 8
                    for hq in range(4):
                        h = 4 * qd + hq
                        ob = 4 + h // 7
                        oo = pb[ob][:, (h % 7) * 65:(h % 7) * 65 + 65]
                        pv(ob, oo, PT[:, gi, hq * 128:(hq + 1) * 128], VR[ks][:, h, :], [f"pt{gi}", f"vra{ks}", f"vr1_{ks}"])

                ng = len(groups)
                for gi in range(ng):
                    s_group(gi)
                    if gi >= 1:
                        pv_group(gi - 1)
                    if mid is not None and gi in mid:
                        mid[gi]()
                pv_group(ng - 1)

            def attn_swa(b, hook=None):
                qs = b % NQR
                p_first = [pi for pi, B in enumerate(BOUNDS) if b == B]
                p_last = [pi for pi, B in enumerate(BOUNDS) if b == B - 1]
                groups = [(g, j) for g in range(2) for j in range(3)]

                def s_group(gi):
                    g, j = groups[gi]
                    kb = b - 1 + j
                    ks = kb % 8
                    S, _, bkey = scr()
                    extra = []
                    if j == 0:
                        extra.append((ident[:], cmask[:, 0, :], ["ident", "cmask0"]))
                        if p_first:
                            extra.append((seli[:, 2 * p_first[0] + 1, :], NEG512, ["seli", "cmask2"]))
                    if j == 2:
                        extra.append((ident[:], cmask[:, 1, :], ["ident", "cmask1"]))
                        if p_last:
                            extra.append((seli[:, 2 * p_last[0] + 1, :], NEG512, ["seli", "cmask2"]))
                    qsrc = (QSA if g == 0 else QSB)[qs][:].rearrange("p a t -> p (a t)")
                    P.op("pe", lambda e, S=S, ks=ks, qsrc=qsrc, last=(len(extra) == 0): e.matmul(S[:], lhsT=KR[ks][:, 4, :], rhs=qsrc, start=True, stop=last),
                         reads=[f"kr{ks}b", f"qr{qs}c", f"qr{qs}d"], writes=[bkey])
                    for xi, (lt, rh, rk) in enumerate(extra):
                        P.op("pe", lambda e, S=S, lt=lt, rh=rh, last=(xi == len(extra) - 1): e.matmul(S[:], lhsT=lt, rhs=rh, start=False, stop=last),
                             reads=rk, writes=[bkey])
                    P.op("act", lambda e, S=S, gi=gi: e.activation(out=PTS[:, gi, :], in_=S[:], func=AF.Exp),
                         reads=[bkey], writes=[f"pts{gi}"])

                def pv_group(gi):
                    g, j = groups[gi]
                    ks = (b - 1 + j) % 8
                    for a in range(4):
                        hh = 8 + 4 * g + a
                        ob = 4 + hh // 7
                        oo = pb[ob][:, (hh % 7) * 65:(hh % 7) * 65 + 65]
                        pv(ob, oo, PTS[:, gi, a * 128:(a + 1) * 128], VR[ks][:, 8 + g, :], [f"pts{gi}", f"vrs{ks}", f"vr1_{ks}"])

                for gi in range(6):
                    s_group(gi)
                    if gi >= 1:
                        pv_group(gi - 1)
                    if hook is not None:
                        hook(gi)
                pv_group(5)

            def attn_norm(b):
                ai = b % 2
                at = ATT[ai].rearrange("p (h d) -> p h d", d=64)
                for ob, h0, nh in ((4, 0, 7), (5, 7, 7), (6, 14, 2)):
                    ov = pb[ob][:, 0:nh * 65].rearrange("p (h d) -> p h d", d=65)
                    dn = den[:, h0:h0 + nh]
                    P.op("dve", lambda e, ov=ov, dn=dn, h0=h0, nh=nh: e.tensor_tensor(out=dn, in0=ov[:, :, 64], in1=esink[:, l * 16 + h0:l * 16 + h0 + nh], op=ALU.add),
                         reads=[f"pb{ob}", "esink"], writes=[f"den{ob}"])
                    P.op("dve", lambda e, dn=dn: e.reciprocal(out=dn, in_=dn), reads=[f"den{ob}"], writes=[f"den{ob}"])
                    P.op("dve", lambda e, ov=ov, dn=dn, h0=h0, nh=nh: e.tensor_tensor(out=at[:, h0:h0 + nh, :], in0=ov[:, :, 0:64],
                                                                                     in1=dn.unsqueeze(2).to_broadcast([128, nh, 64]), op=ALU.mult),
                         reads=[f"pb{ob}", f"den{ob}"], writes=[f"att{ai}_{ob}"])

            def attn_xo_load(b):
                xi = b % 2
                P.op("sp", lambda e: e.dma_start(out=XO[xi], in_=src[b * 128:(b + 1) * 128, :]),
                     reads=[srckey(l, b)], writes=[f"xo{xi}"], dma_slot=f"xol{xi}")

            def attn_out_t(b):
                ai = b % 2
                _, tv, tk = tscr()
                for kc in range(8):
                    P.op("pe", lambda e, kc=kc: e.transpose(out=tv[:, kc, :], in_=ATT[ai][:, kc * 128:(kc + 1) * 128], identity=ident[:]),
                         reads=[f"att{ai}_4", f"att{ai}_5", f"att{ai}_6", "ident"], writes=[tk])
                P.op("dve", lambda e: e.tensor_copy(out=ATT_T[ai], in_=tv), reads=[tk], writes=[f"attT{ai}"])

            def attn_out_mm(b, hf):
                ai, xi = b % 2, b % 2
                pp = pb[hf]
                for kc in range(8):
                    P.op("pe", lambda e, kc=kc: e.matmul(pp[:], lhsT=ATT_T[ai][:, kc, :], rhs=WOUT[:, kc, hf * 512:(hf + 1) * 512],
                                                         start=(kc == 0), stop=(kc == 7)),
                         reads=[f"attT{ai}", f"wout{kc}"], writes=[f"pb{hf}"])
                P.op("dve", lambda e: e.tensor_tensor(out=XO[xi][:, hf * 512:(hf + 1) * 512], in0=pp[:], in1=XO[xi][:, hf * 512:(hf + 1) * 512], op=ALU.add),
                     reads=[f"pb{hf}", f"xo{xi}"], writes=[f"xo{xi}"])

            def attn_out_store(b):
                xi = b % 2
                P.op("pool", lambda e: e.dma_start(out=xs[b * 128:(b + 1) * 128, :], in_=XO[xi]),
                     reads=[f"xo{xi}"], writes=[f"xs{b}"], dma_slot=f"xos{xi}")

            def attn_out(b):
                attn_out_mm(b, 0)
                attn_out_mm(b, 1)
                attn_out_store(b)

            def proj(p):
                proj_a(p); proj_b(p); proj_c(p); proj_d(p)

            LA = 4
            pro = list(range(klo, min(lo + LA, khi)))
            seq = pro + ([lo + LA] if lo + LA < khi else [])
            proj_a(seq[0])
            proj_b(seq[0])
            for i, pp_ in enumerate(pro):
                nxt = seq[i + 1] if i + 1 < len(seq) else None
                if nxt is not None:
                    proj_a(nxt)
                proj_c(pp_, early_d1=True)
                if nxt is not None and nxt in pro:
                    proj_b(nxt)
                proj_d2(pp_)
            pend_d2 = [None]
            for b in range(lo, hi):
                p = b + LA
                dop = p < khi
                if dop:
                    proj_b(p)
                mid = {}
                if pend_d2[0] is not None:
                    mid[1] = (lambda q=pend_d2[0]: proj_d2(q))
                    pend_d2[0] = None
                if b > lo:
                    mid[3] = (lambda b=b: attn_out_t(b - 1))
                    mid[6] = (lambda b=b: attn_out_mm(b - 1, 0))
                    mid[8] = (lambda b=b: (attn_out_mm(b - 1, 1), attn_out_store(b - 1)))
                attn_na(b, mid=mid)
                attn_xo_load(b)
                if p + 1 < khi:
                    proj_a(p + 1)
                if dop:
                    proj_c(p, early_d1=True, only=0)
                    attn_swa(b, hook=(lambda gi, p=p: proj_c(p, early_d1=True, only=gi + 1) if gi < 4 else None))
                    pend_d2[0] = p
                    if p == khi - 1:
                        for kc in range(4):
                            P.op("pool", lambda e, kc=kc: e.dma_start(out=WUP[:, kc, :], in_=w_up[l, kc * 128:(kc + 1) * 128, :]),
                                 writes=[f"win{k}" for k in range(8)], dma_slot="wl0")
                        pre_wup[0] = True
                else:
                    attn_swa(b)
                attn_norm(b)
            if pend_d2[0] is not None:
                proj_d2(pend_d2[0])
            attn_out_t(hi - 1)
            attn_out(hi - 1)

        def phase2(l, last):
            lo, hi = 2 + 2 * (l + shift), 62 - 2 * (l + shift)
            dm = []
            for kc in range(8):
                dm.append((WUP[:, kc, :], w_up[l, kc * 128:(kc + 1) * 128, :]))
            for hf in range(2):
                for h4 in range(8):
                    dm.append((WDN[:, 4 * h4:4 * h4 + 4, hf * 512:(hf + 1) * 512],
                               w_down[l, h4 * 512:(h4 + 1) * 512, hf * 512:(hf + 1) * 512].rearrange("(k p) n -> p k n", p=128)))
            wload(dm[4:8] if pre_wup[0] else dm[:8], [f"wup{kc}" for kc in range(8)], "wl0")
            pre_wup[0] = False
            wload(dm[8:16], [f"wdn{hc}_0" for hc in range(32)], "wl1")
            wload(dm[16:], [f"wdn{hc}_1" for hc in range(32)], "wl2")
            P.op("sp", lambda e: e.dma_start(out=GB, in_=gmlp_d[l].partition_broadcast(128)), writes=["gb", "stg0"], dma_slot="c1")
            if last:
                P.op("sp", lambda e: e.dma_start(out=GFIN, in_=gfin_d.partition_broadcast(128)), writes=["gfin"], dma_slot="c3")

            sbs = list(range(lo, hi, 2))

            def mlp_a0(si):
                b0 = sbs[si]
                xi = si % 2
                P.op("sp", lambda e: e.dma_start(out=XM[xi], in_=xs[b0 * 128:(b0 + 2) * 128, :].rearrange("(k p) n -> p k n", p=128)),
                     reads=[f"xs{b0}", f"xs{b0 + 1}"], writes=[f"xm{xi}"], dma_slot=f"xml{xi}")

            def mlp_a1(si):
                xi = si % 2
                for k in range(2):
                    sc = stat[:, 4 + k:5 + k]
                    rc = stat[:, 6 + k:7 + k]
                    P.op("act", lambda e, k=k, sc=sc: e.activation(out=H2[k], in_=XM[xi][:, k, :], func=AF.Square, accum_out=sc),
                         reads=[f"xm{xi}"], writes=[f"h2{k}", f"ss2{k}"])
                    P.op("act", lambda e, sc=sc, rc=rc: e.activation(out=rc, in_=sc, func=AF.Ln, scale=1.0 / D, bias=EPS),
                         reads=[f"ss2{k}"], writes=[f"rs2{k}"])
                    P.op("act", lambda e, rc=rc: e.activation(out=rc, in_=rc, func=AF.Exp, scale=-0.5),
                         reads=[f"rs2{k}"], writes=[f"rs2{k}"])
                    P.op("dve", lambda e, k=k, rc=rc: e.scalar_tensor_tensor(out=H2[k], in0=XM[xi][:, k, :], scalar=rc, in1=GB, op0=ALU.mult, op1=ALU.mult),
                         reads=[f"xm{xi}", f"rs2{k}", "gb"], writes=[f"h2{k}"])

            def mlp_a2(si):
                ti = si % 2
                for k in range(2):
                    for kc in range(8):
                        P.op("pe", lambda e, kc=kc, k=k: e.transpose(out=psT[:, kc, :], in_=H2[k][:, kc * 128:(kc + 1) * 128], identity=ident[:]),
                             reads=[f"h2{k}", "ident"], writes=["psT"])
                    P.op("dve", lambda e, k=k: e.tensor_copy(out=H2T[ti][:, :, k * 128:(k + 1) * 128], in_=psT[:]), reads=["psT"], writes=[f"h2t{ti}"])

            def mlp_up(si, hooks=None):
                for hc in range(32):
                    if hooks and hc in hooks:
                        hooks[hc]()
                    bank = hc % 3
                    U = pb[bank]
                    for kc in range(8):
                        P.op("pe", lambda e, kc=kc, U=U, hc=hc: e.matmul(U[:, 0:256], lhsT=WUP[:, kc, hc * 128:(hc + 1) * 128], rhs=H2T[si % 2][:, kc, :],
                                                                        start=(kc == 0), stop=(kc == 7)),
                             reads=[f"wup{kc}", f"h2t{si % 2}"], writes=[f"pb{bank}"])
                    ri = hc % 2
                    P.op("act", lambda e, U=U, ri=ri: e.activation(out=RTMP[ri], in_=U[:, 0:256], func=AF.Relu),
                         reads=[f"pb{bank}"], writes=[f"rtmp{ri}"])
                    P.op("pool", lambda e, ri=ri, hc=hc: e.tensor_tensor(out=HID[:, hc, :], in0=RTMP[ri], in1=RTMP[ri], op=ALU.mult),
                         reads=[f"rtmp{ri}"], writes=[f"hid{hc}"])

            def mlp_down(si):
                b0 = sbs[si]
                xi = si % 2
                for hf in range(2):
                    for k in range(2):
                        bank = 3 + k * 2 + hf
                        Dp = pb[bank]
                        for hc in range(32):
                            P.op("pe", lambda e, hc=hc, Dp=Dp, k=k, hf=hf: e.matmul(Dp[:], lhsT=HID[:, hc, k * 128:(k + 1) * 128], rhs=WDN[:, hc, hf * 512:(hf + 1) * 512],
                                                                                  start=(hc == 0), stop=(hc == 31)),
                                 reads=[f"hid{hc}", f"wdn{hc}_{hf}"], writes=[f"pb{bank}"])
                        xsl = XM[xi][:, k, hf * 512:(hf + 1) * 512]
                        P.op("dve", lambda e, Dp=Dp, xsl=xsl: e.tensor_tensor(out=xsl, in0=Dp[:], in1=xsl, op=ALU.add),
                             reads=[f"pb{bank}", f"xm{xi}"], writes=[f"xm{xi}"])
                if not last:
                    P.op("pool", lambda e: e.dma_start(out=xs[b0 * 128:(b0 + 2) * 128, :].rearrange("(k p) n -> p k n", p=128), in_=XM[xi]),
                         reads=[f"xm{xi}"], writes=[f"xs{b0}", f"xs{b0 + 1}"], dma_slot=f"xms{xi}")
                else:
                    for k in range(2):
                        sc = stat[:, 4 + k:5 + k]
                        rc = stat[:, 6 + k:7 + k]
                        sc = stat[:, k:k + 1]
                        rc = stat[:, 2 + k:3 + k]
                        P.op("act", lambda e, k=k, sc=sc: e.activation(out=H2[k], in_=XM[xi][:, k, :], func=AF.Square, accum_out=sc),
                             reads=[f"xm{xi}"], writes=[f"h2{k}", f"fss{k}"])
                        P.op("act", lambda e, sc=sc, rc=rc: e.activation(out=rc, in_=sc, func=AF.Ln, scale=1.0 / D, bias=EPS),
                             reads=[f"fss{k}"], writes=[f"frs{k}"])
                        P.op("act", lambda e, rc=rc: e.activation(out=rc, in_=rc, func=AF.Exp, scale=-0.5),
                             reads=[f"frs{k}"], writes=[f"frs{k}"])
                        P.op("dve", lambda e, k=k, rc=rc: e.scalar_tensor_tensor(out=XM[xi][:, k, :], in0=XM[xi][:, k, :], scalar=rc, in1=GFIN,
                                                                                 op0=ALU.mult, op1=ALU.mult),
                             reads=[f"xm{xi}", f"frs{k}", "gfin"], writes=[f"xm{xi}"])
                    yb = b0 - HALO
                    P.op("pool", lambda e: e.dma_start(out=y[yb * 128:(yb + 2) * 128, :].rearrange("(k p) n -> p k n", p=128), in_=XM[xi]),
                         reads=[f"xm{xi}"], writes=[f"y{yb}"], dma_slot=f"xms{xi}")

            n = len(sbs)
            mlp_a0(0)
            mlp_a1(0)
            mlp_a2(0)
            for si in range(n):
                hooks = None
                if si + 1 < n:
                    hooks = {0: (lambda si=si: mlp_a0(si + 1)), 10: (lambda si=si: mlp_a1(si + 1)), 22: (lambda si=si: mlp_a2(si + 1))}
                mlp_up(si, hooks)
                if si == n - 1 and not last:
                    for k2 in range(4):
                        P.op("pool", lambda e, k2=k2: e.dma_start(out=WIN[:, 2 * k2:2 * k2 + 2, :],
                                                                  in_=w_in[l + 1, k2 * 256:(k2 + 1) * 256, :].rearrange("(k p) n -> p k n", p=128)),
                             writes=[f"wup{k}" for k in range(8)], dma_slot="wl0")
                    pre_win[0] = True
                mlp_down(si)

        for l in range(nlayers):
            phase1(l)
            P.barrier()
            phase2(l, last=(l == nlayers - 1))
            P.barrier(final=(l == nlayers - 1))

        names = P.finalize()
        sems = {}
        for k in names:
            nm = "s_" + "_".join(str(t) for t in k)
            sems[k] = es.enter_context(nc.semaphore(nm))
        with nc.Block() as block:
            @block.sync
            def _(e):
                P.run("sp", e, sems)

            @block.tensor
            def _(e):
                P.run("pe", e, sems)

            @block.scalar
            def _(e):
                P.run("act", e, sems)

            @block.vector
            def _(e):
                P.run("dve", e, sems)

            @block.gpsimd
            def _(e):
                P.run("pool", e, sems)
    return nc


def _ubias(rpb_l):
    a = np.arange(2)[:, None, None, None]
    kc = np.arange(64)[None, :, None, None]
    bq = np.arange(2)[None, None, :, None]
    c = np.arange(64)[None, None, None, :]
    cs = np.clip(c - 8, 0, 48)
    colvalid = (kc >= cs) & (kc <= cs + 15)
    dci = np.clip(kc - c + 15, 0, 30) + 0 * a + 0 * bq
    tabs = []
    specs = [(-6, False), (-4, False), (-2, False), (0, False), (2, False), (4, False), (6, False), (-4, True), (4, True)]
    for delta, normal in specs:
        dr = delta + a - bq
        valid = colvalid & (np.abs(dr) <= 7)
        if normal:
            valid = valid & (dr >= -4) & (dr <= 3)
        dri = np.clip(dr + 7, 0, 14) + 0 * kc + 0 * c
        vals = rpb_l[:, dri, dci]
        t = np.where(np.broadcast_to(valid, vals.shape[1:])[None], vals, np.float32(NEGV)).astype(np.float32)
        tabs.append(t.reshape(8, 128, 128))
    U = np.stack(tabs, 1)
    return np.ascontiguousarray(U.transpose(2, 1, 0, 3)).reshape(128, 9, 8 * 128)


def _rope_tables(pos):
    half = 32
    inv = (np.float32(10000.0) ** (-(np.arange(half, dtype=np.float32) / np.float32(half)))).astype(np.float32)
    ang = pos.astype(np.float32)[:, None] * inv[None, :]
    cos = np.cos(ang).astype(np.float32)
    sin = np.sin(ang).astype(np.float32)
    cc = np.concatenate([cos, cos], 1)
    ss = np.concatenate([-sin, sin], 1)
    q = np.float32(0.125)
    return np.stack([cc * q, ss * q, cc, ss], 1).astype(np.float32)


def _consts():
    k = np.arange(128)[:, None]
    q = np.arange(128)[None, :]
    prev = np.where(k >= q, 0.0, NEGV).astype(np.float32)
    nxt = np.where(k <= q, 0.0, NEGV).astype(np.float32)
    neg = np.full((128, 512), NEGV, np.float32)
    cm = np.concatenate([np.tile(prev, (1, 4)), np.tile(nxt, (1, 4)), neg], 1)
    return np.ascontiguousarray(cm), np.eye(128, dtype=np.float32)


_NC_CACHE = {}


def kernel(x_prompt, x_sample, norm_mix, w_in, rpb, sink, w_out, norm_mlp, w_up, w_down, norm_final):
    f32 = np.float32
    xa = np.concatenate([np.asarray(x_prompt, f32).reshape(-1, D), np.asarray(x_sample, f32).reshape(-1, D)], 0)
    xpad = np.zeros((T_ALL + 2 * HALO * 128, D), f32)
    xpad[HALO * 128:HALO * 128 + T_ALL] = xa
    w_in = np.array(np.asarray(w_in, f32))
    qsw = w_in[:, :, 1536:2048].reshape(NLAYERS, D, 8, 64)
    w_in[:, :, 1536:2048] = qsw[:, :, [0, 4, 1, 5, 2, 6, 3, 7], :].reshape(NLAYERS, D, 512)
    w_out = np.ascontiguousarray(np.asarray(w_out, f32))
    w_up = np.ascontiguousarray(np.asarray(w_up, f32))
    w_down = np.ascontiguousarray(np.asarray(w_down, f32))
    gmix = np.ascontiguousarray(np.asarray(norm_mix, f32))
    gmlp = np.ascontiguousarray(np.asarray(norm_mlp, f32))
    gfin = np.ascontiguousarray(np.asarray(norm_final, f32))
    rpb = np.asarray(rpb, f32)
    ubias = np.stack([_ubias(rpb[l]) for l in range(NLAYERS)], 0)
    sink16 = np.full((NLAYERS, 16), -1e4, f32)
    sink16[:, 8:] = np.asarray(sink, f32)
    sink16 = sink16.reshape(-1)
    cmask, ident = _consts()
    g = np.arange(-HALO * 128, T_ALL + HALO * 128)
    pos = np.zeros_like(g)
    ends = list(SEQ_STARTS[1:]) + [T_ALL]
    for s0, s1 in zip(SEQ_STARTS, ends):
        m = (g >= s0) & (g < s1)
        pos[m] = g[m] - s0
    rope_all = _rope_tables(pos)
    bset = set(SEQ_STARTS) | {T_ALL}
    in_maps = []
    for c in range(8):
        g0 = 6144 * c
        seli = np.zeros((128, 8, 128), f32)
        for p, B in enumerate(BOUNDS):
            isb = (g0 + (B - HALO) * 128) in bset
            seli[:, 2 * p + (1 if isb else 0), :] = np.eye(128, dtype=f32)
        in_maps.append({
            "x_ext": xpad[g0:g0 + NB * 128],
            "w_in": w_in, "w_out": w_out, "w_up": w_up, "w_down": w_down,
            "gmix": gmix, "gmlp": gmlp, "gfin": gfin, "ubias": ubias, "sink16": sink16,
            "rope": np.ascontiguousarray(rope_all[g0:g0 + NB * 128].reshape(NB, 128, 256)),
            "cmask": cmask, "seli": seli.reshape(128, 1024), "ident": ident,
        })
    nl = int(os.environ.get("KNL", NLAYERS))
    if nl not in _NC_CACHE:
        _NC_CACHE[nl] = build_program(nl)
    res = run_bass_kernel_spmd(_NC_CACHE[nl], in_maps, core_ids=list(range(8)))
    yall = np.concatenate([np.asarray(r["y"], f32) for r in res.results], 0)
    y_prompt = yall[:16384].reshape(2, 8192, D)
    y_sample = yall[16384:].reshape(2, 16384, D)
    return (y_prompt, y_sample)
```
